# Optimizing a Trainium2 kernel written in Bass

```python
import jax, jax.numpy as jnp
from jax import lax
import numpy as np

D_MODEL = 1024
BATCH = 16
SEQ = 4096
DEPTH = 1

PLE_DIM = 256
CHUNK = 128
A_WIDTH = 1024
A_GROUPS = 8
A_GROUP_DIM = A_WIDTH // A_GROUPS
B_WIDTH = 1024
B_CONV = 31
FFN_DIM = 2816
FFN_CONV = 3
IN_COLS = 2 * A_WIDTH + 2 * B_WIDTH + 2 * D_MODEL
EPS_RMS = 1e-6
EPS_LN = 1e-5

kernel_name = "hybrid_gmlp_conformer_gated_block"


def rmsnorm(x, g):
    xf = x.astype(jnp.float32)
    y = xf * lax.rsqrt(jnp.mean(xf * xf, axis=-1, keepdims=True) + EPS_RMS)
    return y.astype(x.dtype) * g


def layernorm(x, g, b):
    xf = x.astype(jnp.float32)
    mu = jnp.mean(xf, axis=-1, keepdims=True)
    var = jnp.mean(jnp.square(xf - mu), axis=-1, keepdims=True)
    y = (xf - mu) * lax.rsqrt(var + EPS_LN)
    return y.astype(x.dtype) * g + b


def causal_dwconv(x, w, b):
    k, c = w.shape
    y = lax.conv_general_dilated(
        x, w[:, None, :], window_strides=(1,), padding=((k - 1, 0),),
        dimension_numbers=("NWC", "WIO", "NWC"), feature_group_count=c)
    return y + b


def setup_inputs(seed: int = 0) -> dict:
    key = jax.random.key(seed)
    ks = jax.random.split(key, 32)
    f32 = jnp.float32

    def nrm(k, shape, scale):
        return jax.random.normal(k, shape, f32) * scale

    def gain(k, shape):
        return 1.0 + 0.01 * jax.random.normal(k, shape, f32)

    L = DEPTH
    return {
        "x": jax.random.normal(ks[0], (BATCH, SEQ, D_MODEL), f32),
        "p": jax.random.normal(ks[1], (DEPTH, BATCH, SEQ, PLE_DIM), f32),
        "g_mix": gain(ks[2], (L, D_MODEL)),
        "w_in": nrm(ks[3], (L, D_MODEL, IN_COLS), D_MODEL ** -0.5),
        "ln_v_g": gain(ks[4], (L, A_WIDTH)),
        "ln_v_b": nrm(ks[5], (L, A_WIDTH), 0.01),
        "w_s": nrm(ks[6], (L, A_GROUPS, CHUNK, CHUNK), CHUNK ** -0.5),
        "b_s": 1.0 + 0.1 * jax.random.normal(ks[7], (L, A_GROUPS, CHUNK), f32),
        "w_a_out": nrm(ks[8], (L, A_WIDTH, D_MODEL), A_WIDTH ** -0.5),
        "conv_b_w": nrm(ks[9], (L, B_CONV, B_WIDTH), B_CONV ** -0.5),
        "conv_b_b": nrm(ks[10], (L, B_WIDTH), 0.01),
        "ln_b_g": gain(ks[11], (L, B_WIDTH)),
        "ln_b_b": nrm(ks[12], (L, B_WIDTH), 0.01),
        "w_b_out": nrm(ks[13], (L, B_WIDTH, D_MODEL), B_WIDTH ** -0.5),
        "w_o": nrm(ks[14], (L, D_MODEL, D_MODEL), D_MODEL ** -0.5),
        "g_ffn": gain(ks[15], (L, D_MODEL)),
        "w_up": nrm(ks[16], (L, D_MODEL, 2 * FFN_DIM), D_MODEL ** -0.5),
        "ffn_conv_w": nrm(ks[17], (L, FFN_CONV, 2 * FFN_DIM), FFN_CONV ** -0.5),
        "ffn_conv_b": nrm(ks[18], (L, 2 * FFN_DIM), 0.01),
        "w_down": nrm(ks[19], (L, FFN_DIM, D_MODEL), FFN_DIM ** -0.5),
        "g_pg": gain(ks[20], (L, D_MODEL)),
        "w_pg": nrm(ks[21], (L, D_MODEL, D_MODEL), D_MODEL ** -0.5),
        "w_ple": nrm(ks[22], (L, PLE_DIM, D_MODEL), PLE_DIM ** -0.5),
        "g_ple": gain(ks[23], (L, D_MODEL)),
        "g_final": gain(ks[24], (D_MODEL,)),
    }


def reference(x, p, g_mix, w_in, ln_v_g, ln_v_b, w_s, b_s, w_a_out,
              conv_b_w, conv_b_b, ln_b_g, ln_b_b, w_b_out, w_o,
              g_ffn, w_up, ffn_conv_w, ffn_conv_b, w_down,
              g_pg, w_pg, w_ple, g_ple, g_final):
    bsz, seq = x.shape[0], x.shape[1]
    n_chunks = seq // CHUNK
    tril = jnp.tril(jnp.ones((CHUNK, CHUNK), dtype=bool))
    splits = [A_WIDTH, 2 * A_WIDTH, 2 * A_WIDTH + B_WIDTH,
              2 * A_WIDTH + 2 * B_WIDTH, 2 * A_WIDTH + 2 * B_WIDTH + D_MODEL]

    for i in range(DEPTH):
        h = rmsnorm(x, g_mix[i])
        z = h @ w_in[i]
        u, v, a_b, gl_b, gate_a, gate_b = jnp.split(z, splits, axis=-1)

        u = jax.nn.gelu(u)
        v = layernorm(jax.nn.gelu(v), ln_v_g[i], ln_v_b[i])
        vr = v.reshape(bsz, n_chunks, CHUNK, A_GROUPS, A_GROUP_DIM)
        ws = jnp.where(tril[None], w_s[i], jnp.zeros_like(w_s[i]))
        mix = jnp.einsum("gts,bnsgc->bntgc", ws, vr) + b_s[i].T[:, :, None]
        y_a = (u * mix.reshape(bsz, seq, A_WIDTH)) @ w_a_out[i]

        glu = a_b * jax.nn.sigmoid(gl_b)
        c = causal_dwconv(glu, conv_b_w[i], conv_b_b[i])
        c = jax.nn.silu(layernorm(c, ln_b_g[i], ln_b_b[i]))
        y_b = c @ w_b_out[i]

        merged = jax.nn.sigmoid(gate_a) * y_a + jax.nn.sigmoid(gate_b) * y_b
        x = x + merged @ w_o[i]

        h = rmsnorm(x, g_ffn[i])
        up = causal_dwconv(h @ w_up[i], ffn_conv_w[i], ffn_conv_b[i])
        gate, val = jnp.split(up, 2, axis=-1)
        x = x + (jax.nn.gelu(gate) * val) @ w_down[i]

        pe = rmsnorm(p[i] @ w_ple[i], g_ple[i])
        pg = jax.nn.sigmoid(rmsnorm(x, g_pg[i]) @ w_pg[i])
        x = x + pe * pg

    return rmsnorm(x, g_final)
```

```python
import numpy as np
import concourse.bass as bass
import concourse.mybir as mybir
from concourse.bass_utils import run_bass_kernel_spmd

F32 = mybir.dt.float32
BF16 = mybir.dt.bfloat16
U8 = mybir.dt.uint8
AF = mybir.ActivationFunctionType
ALU = mybir.AluOpType

D = 1024
FF = 2816
NFC = 22
T = 512
EPS_RMS = 1e-6
EPS_LN = 1e-5
SAME_ENGINE_SYNC = True
OPT_TINY = True
OPT_CAST = True
OPT_SQ = False
OPT_D = False
NSLOT = 5
NSTG = 2

CV_GMIX, CV_GFFN, CV_GPG, CV_LNBG, CV_LNBB, CV_CBB = 0, 8, 16, 24, 32, 40
CV_CBW = 48
CV_FCW = CV_CBW + 8 * 31
CV_FCB = CV_FCW + 3 * 44
CV_LNVG = CV_FCB + 44
CV_LNVB = CV_LNVG + 8
NCV = CV_LNVB + 8


class Buf:
    def __init__(self, arena, off, shape, dt, esz, parts=128):
        self.off, self.shape, self.esz = off, tuple(shape), esz
        n = int(np.prod(shape))
        self.nbytes = n * esz
        ap = arena[0:parts, off:off + n * esz].bitcast(dt)
        if len(shape) == 2:
            ap = ap.rearrange("p (a b) -> p a b", a=shape[0])
        elif len(shape) == 3:
            ap = ap.rearrange("p (a b c) -> p a b c", a=shape[0], b=shape[1])
        self.ap = ap
        self.slot_bytes = self.nbytes // shape[0] if len(shape) >= 2 else self.nbytes

    def __getitem__(self, k):
        return self.ap[k]

    def r(self, i=None, n=1):
        if i is None:
            return (self.off, self.nbytes)
        return (self.off + i * self.slot_bytes, n * self.slot_bytes)


def _snap(fn):
    cl = fn.__closure__ or ()
    fn._snap = [c.cell_contents for c in cl]
    return fn


def _check_snap(fn):
    cl = fn.__closure__ or ()
    for c, v in zip(cl, fn._snap):
        w = c.cell_contents
        assert w is v or (type(v) in (int, float, str, bool) and w == v) or repr(w) == repr(v), ("late-bound closure variable changed", fn.__code__.co_firstlineno)


class Prog:
    def __init__(self, small_base):
        self.ops = {e: [] for e in ("pe", "act", "dve", "pool", "sp")}
        self.cnt = {e: 0 for e in ("pe", "act", "dve", "pool")}
        self.waited = {e: {} for e in self.ops}
        self.lastw = {}
        self.readers = {}
        self.dcnt = {}
        self.small_base = small_base
        self.pe_names = []

    def cells(self, rng):
        if isinstance(rng, tuple) and isinstance(rng[0], str):
            return [rng]
        off, n = rng
        g = 32 if off >= self.small_base else 256
        return [("sb", g, c) for c in range(off // g, (off + n - 1) // g + 1)]

    def _deps(self, eng, reads, writes):
        need = {}

        def add(ev):
            if ev is None:
                return
            sem, val, src = ev
            if src == eng and (eng == "pe" or not SAME_ENGINE_SYNC):
                return
            if need.get(sem, 0) < val:
                need[sem] = val
        for r in reads:
            for c in self.cells(r):
                add(self.lastw.get(c))
        for w in writes:
            for c in self.cells(w):
                add(self.lastw.get(c))
                for ev in self.readers.get(c, ()):
                    add(ev)
        out = []
        wd = self.waited[eng]
        for sem, val in need.items():
            if wd.get(sem, 0) < val:
                wd[sem] = val
                out.append((sem, val))
        return out

    def _commit(self, ev, reads, writes):
        for r in reads:
            for c in self.cells(r):
                self.readers.setdefault(c, []).append(ev)
        for w in writes:
            for c in self.cells(w):
                self.lastw[c] = ev
                self.readers[c] = []

    def op(self, eng, fns, reads, writes, fresh=False, tag=None):
        if not isinstance(fns, (list, tuple)):
            fns = [fns]
        if fresh:
            for w in writes:
                for c in self.cells(w):
                    assert c not in self.lastw or self.readers.get(c), ("overwriting unread data", c)
        waits = self._deps(eng, reads, writes)
        self.cnt[eng] += 1
        ev = (eng, self.cnt[eng], eng)
        self.ops[eng].append((waits, [_snap(f) for f in fns], (eng, 1), tag))
        self._commit(ev, reads, writes)

    def dma(self, sem, fn, reads, writes, eng="sp"):
        waits = self._deps(eng, reads, writes)
        self.dcnt[sem] = self.dcnt.get(sem, 0) + 16
        ev = (sem, self.dcnt[sem], "dma")
        self.ops[eng].append((waits, [_snap(fn)], (sem, 16), None))
        self._commit(ev, reads, writes)
        return ev

    def wait_all(self, eng, evs):
        waits = []
        wd = self.waited[eng]
        for sem, val, _ in evs:
            if wd.get(sem, 0) < val:
                wd[sem] = val
                waits.append((sem, val))
        self.ops[eng].append((waits, [], None, None))


def build_nc(n_seq, tps):
    ntile = n_seq * tps
    ntok = ntile * T
    nc = bass.Bass("TRN2", target_bir_lowering=False)

    x_d = nc.dram_tensor("x", [ntok, D], F32, kind="ExternalInput").ap()
    p_d = nc.dram_tensor("p", [ntok, 256], F32, kind="ExternalInput").ap()
    w_in_d = nc.dram_tensor("w_in", [D, 6144], F32, kind="ExternalInput").ap()
    w_a_d = nc.dram_tensor("w_a_out", [D, D], F32, kind="ExternalInput").ap()
    w_b_d = nc.dram_tensor("w_b_out", [D, D], F32, kind="ExternalInput").ap()
    w_o_d = nc.dram_tensor("w_o", [D, D], F32, kind="ExternalInput").ap()
    w_up_d = nc.dram_tensor("w_up", [D, 2 * FF], F32, kind="ExternalInput").ap()
    w_dn_d = nc.dram_tensor("w_down", [FF, D], F32, kind="ExternalInput").ap()
    w_pg_d = nc.dram_tensor("w_pg", [D, D], F32, kind="ExternalInput").ap()
    w_ple_d = nc.dram_tensor("w_ple", [256, D], F32, kind="ExternalInput").ap()
    cv_d = nc.dram_tensor("cv", [128, NCV], F32, kind="ExternalInput").ap()
    rows_d = nc.dram_tensor("rows", [4, D], F32, kind="ExternalInput").ap()
    bs_d = nc.dram_tensor("bs", [1, 1024], F32, kind="ExternalInput").ap()
    wst_d = nc.dram_tensor("wst", [128, 8, 128], F32, kind="ExternalInput").ap()
    out_d = nc.dram_tensor("out", [ntok, D], F32, kind="ExternalOutput").ap()
    NBLK = 44
    scr_d = nc.dram_tensor("wscr", [NBLK, 128, 4096], BF16, kind="Internal").ap()

    total = nc.sbuf_bytes_remaining - 64
    total -= total % 64
    arena = nc.alloc_sbuf_tensor("arena", [128, total], U8)
    cur = [0]

    def alloc(nbytes):
        o = cur[0]
        cur[0] += (nbytes + 511) // 512 * 512
        return o

    def buf(shape, dt, off=None):
        esz = 4 if dt == F32 else 2
        n = int(np.prod(shape)) * esz
        if off is None:
            off = alloc(n)
        return Buf(arena, off, shape, dt, esz)

    XBS = [buf([4, 1024], F32) for _ in range(2)]
    HT = buf([8, 512], BF16)
    WS = [buf([4096], BF16) for _ in range(NSLOT)]
    a1 = alloc(24576)
    GUT = buf([8, 512], BF16, a1)
    UMT = buf([8, 512], BF16, a1 + 8192)
    NBF = buf([4, 1024], BF16, a1 + 16384)
    HMT = buf([22, 512], BF16, a1)
    MGT = buf([8, 512], BF16, a1)
    GLUT = buf([8, 542], BF16)
    a2 = alloc(20480)
    CB = buf([8, 512], F32, a2)
    SQ = buf([2, 512], F32, a2 + 16384)
    PEN = buf([4, 1024], F32, a2)
    SG = buf([2, 512], F32, a2 + 16384)
    STAT = buf([3, 512], F32)
    SCT = buf([8, 512], BF16)
    a3 = alloc(16384)
    T1 = buf([8, 512], F32, a3)
    GV = buf([2, 1024], F32, a3)
    ACC = buf([4, 512], F32, a3)
    GEL = buf([2, 512], F32, a3 + 8192)
    SGP = buf([2, 512], F32, a3 + 12288)
    OUTB = buf([4, 1024], F32, a2)
    HBT = [buf([1024], BF16) for _ in range(2)]
    JUNK = buf([1024], BF16)
    JUNKF = buf([1024], F32)
    PIN = buf([4, 256], F32)
    PBF = buf([4, 256], BF16)
    PT = buf([2, 512], BF16)
    GPLEB = buf([1024], F32)
    GFINB = buf([1024], F32)
    WST = buf([8, 128], BF16)
    IDENT = buf([128], BF16)
    ONESB = buf([128], BF16)
    ONES = buf([128], F32)
    DG = [buf([128], BF16) for _ in range(8)]
    BSB = buf([8, 128], F32)
    small_base = cur[0]
    cur[0] = small_base

    def sbuf_small(shape, dt=F32):
        esz = 4 if dt == F32 else 2
        n = int(np.prod(shape)) * esz
        o = cur[0]
        cur[0] += (n + 31) // 32 * 32
        return Buf(arena, o, shape, dt, esz)

    CV = sbuf_small([NCV])
    MHALF = sbuf_small([8])
    EPSB = sbuf_small([1])
    NSC = {k: [[sbuf_small([1]) for _ in range(3)] for _ in range(4)] for k in ("n1a", "n1b", "n2", "n3", "fin")}
    BN = [sbuf_small([12]) for _ in range(2)]
    MV = [sbuf_small([2]) for _ in range(2)]
    LV = [sbuf_small([2]) for _ in range(2)]
    HAL = sbuf_small([2, 44])
    HC = sbuf_small([2, 44])
    TMPH = sbuf_small([44])
    SSP = sbuf_small([8])
    SP4 = sbuf_small([4])
    assert cur[0] <= total, (cur[0], total)

    ps = nc.alloc_psum_tensor("ps", [128, 4096], F32)

    def bank(b):
        return ps[:, b * 512:(b + 1) * 512]

    def bank_bf(b):
        return ps[:, b * 512:(b + 1) * 512].bitcast(BF16)

    P = Prog(small_base)
    bank_rr = [0]

    def next_bank():
        b = bank_rr[0]
        bank_rr[0] = (b + 1) % 6
        return b

    def cvcol(base, j):
        return CV.ap[:, base + j:base + j + 1]

    def wsrc(w_ap, c0, ncol):
        v = w_ap.rearrange("(kc p) n -> p kc n", p=128)
        return lambda kc0, nkc, _v=v, _c0=c0, _n=ncol: _v[:, kc0:kc0 + nkc, _c0:_c0 + _n]

    blocks = []

    def add_simple(w_ap, c0, gbase=None, nk=8, k0=0):
        src = wsrc(w_ap, c0, 512)
        pieces = []
        kc = 0
        while kc < nk:
            n = min(2, nk - kc)
            pieces.append((kc, n, 0, 512, (lambda a, b, _s=src, _k0=k0: _s(a + _k0, b))))
            kc += n
        blocks.append(dict(nk=nk, ncol=512, pieces=pieces, gbase=gbase))

    for c0 in (1024, 1536, 2048, 3072, 2560, 3584, 0, 512):
        add_simple(w_in_d, c0, CV_GMIX)
    for i in range(2):
        add_simple(w_in_d, 4096 + 512 * i, CV_GMIX)
        add_simple(w_a_d, 512 * i)
    for i in range(2):
        add_simple(w_in_d, 5120 + 512 * i, CV_GMIX)
        add_simple(w_b_d, 512 * i)
    for i in range(2):
        add_simple(w_o_d, 512 * i)
    sple = w_ple_d.rearrange("(kc p) n -> p kc n", p=128)
    blocks.append(dict(nk=2, ncol=1024, gbase=None,
                       pieces=[(kc, 1, 0, 1024, (lambda a, b, _s=sple: _s[:, a:a + b, :])) for kc in range(2)]))
    for i in range(11):
        sg = wsrc(w_up_d, 256 * i, 256)
        sv = wsrc(w_up_d, FF + 256 * i, 256)
        pieces = []
        for kc in range(0, 8, 4):
            pieces.append((kc, 4, 0, 256, sg))
            pieces.append((kc, 4, 256, 256, sv))
        blocks.append(dict(nk=8, ncol=512, pieces=pieces, gbase=CV_GFFN))
    for tsg in range(2):
        for half in range(2):
            for (k0, nk) in ((0, 8), (8, 8), (16, 6)):
                add_simple(w_dn_d, 512 * half, None, nk=nk, k0=k0)
                if tsg == 1:
                    blocks[-1]["scr"] = len(blocks) - 1 - 6
    for i in range(2):
        add_simple(w_pg_d, 512 * i, CV_GPG)
    assert len(blocks) == NBLK

    sem_names = ["pe", "act", "dve", "pool", "cst", "xl0", "xl1", "pl", "os"] + \
        ["wl%d" % i for i in range(NSLOT)] + ["cvb%d" % i for i in range(NBLK)]
    sems = {}

    nblk_total = NBLK * ntile
    wstate = dict(issued=0, released=-1, stg=0, ceng=0)

    def slot_view(s, blk):
        nk, ncol = blk["nk"], blk["ncol"]
        return WS[s].ap[:, 0:nk * ncol].rearrange("p (k n) -> p k n", k=nk)

    conv_issued = [0]

    def record_convert(b):
        blk = blocks[b]
        if blk.get("scr", b) != b:
            return
        nk, ncol = blk["nk"], blk["ncol"]
        dstv = scr_d[b, :, 0:nk * ncol].rearrange("p (k n) -> p k n", k=nk)
        seen = set()
        for (kc0, nkc, col0, ncl, src) in blk["pieces"]:
            if (col0, ncl) in seen:
                continue
            seen.add((col0, ncl))
            sap = src(0, nk) if blk["nk"] != 2 or ncol != 1024 else src(0, 2)
            P.dma("cvb%d" % b, (lambda e, _o=dstv[:, :, col0:col0 + ncl], _i=sap: e.dma_start(out=_o, in_=_i)),
                  reads=[], writes=[], eng="pool")
        P.lastw[("dr", b)] = ("cvb%d" % b, P.dcnt["cvb%d" % b], "dma")
        P.readers[("dr", b)] = []

    def pump_convert(upto):
        while conv_issued[0] < min(upto, NBLK):
            record_convert(conv_issued[0])
            conv_issued[0] += 1

    def record_load(n):
        tile_i, b = divmod(n, NBLK)
        s = n % NSLOT
        blk = blocks[b]
        nel = blk["nk"] * blk["ncol"]
        srange = (WS[s].off, nel * 2)
        if tile_i == 0:
            pump_convert(b + 9)
        sb = blk.get("scr", b)
        P.dma("wl%d" % s, (lambda e, _o=WS[s].ap[:, 0:nel], _i=scr_d[sb, :, 0:nel]: e.dma_start(out=_o, in_=_i)),
              reads=[("dr", sb)], writes=[srange])

    def pump():
        while wstate["issued"] < nblk_total and wstate["issued"] <= wstate["released"] + NSLOT:
            record_load(wstate["issued"])
            wstate["issued"] += 1

    wnext = [0]

    def get_block():
        n = wnext[0]
        wnext[0] += 1
        pump()
        assert wstate["issued"] > n, "weight block not loadable (slot not released)"
        blk = blocks[n % NBLK]
        s = n % NSLOT
        return n, slot_view(s, blk), (WS[s].off, blk["nk"] * blk["ncol"] * 2)

    def release(n):
        assert n == wstate["released"] + 1, (n, wstate["released"])
        wstate["released"] = n
        pump()

    def mm_group(bk, pairs, reads, out_ap=None, first=True, last=True, tag=None):
        o = bank(bk) if out_ap is None else out_ap
        fns = []
        n = len(pairs)
        for i, (l, r) in enumerate(pairs):
            fns.append(lambda e, _l=l, _r=r, _s=(first and i == 0), _t=(last and i == n - 1), _o=o:
                       e.matmul(_o, lhsT=_l, rhs=_r, start=_s, stop=_t))
        P.op("pe", fns, reads=reads, writes=[("ps", bk)], fresh=first, tag=tag)

    def norm_stats(xbuf, ts, kind):
        ss, t4, rs = NSC[kind][ts]
        P.op("act", (lambda e: e.activation(out=JUNK.ap, in_=xbuf[:, ts, :], func=AF.Square, accum_out=ss.ap)),
             reads=[xbuf.r(ts)], writes=[JUNK.r(), ss.r()])
        P.op("dve", (lambda e: e.tensor_scalar(out=t4.ap, in0=ss.ap, scalar1=1.0 / D, scalar2=EPS_RMS,
                                               op0=ALU.mult, op1=ALU.add)), reads=[ss.r()], writes=[t4.r()])
        P.op("pool", (lambda e: e.tensor_tensor(out=rs.ap, in0=t4.ap, in1=MHALF.ap[:, 0:1], op=ALU.pow)),
             reads=[t4.r(), MHALF.r()], writes=[rs.r()])

    def ht_ts(ts):
        return [(HT.off + c * 1024 + ts * 256, 256) for c in range(8)]

    def norm_scale(xbuf, ts, kind):
        rs = NSC[kind][ts][2]
        hb = HBT[ts % 2]
        P.op("dve", (lambda e: e.tensor_scalar(out=hb.ap, in0=xbuf[:, ts, :], scalar1=rs.ap, scalar2=None, op0=ALU.mult)),
             reads=[xbuf.r(ts), rs.r()], writes=[hb.r()])

    def norm_apply(xbuf, ts, kind, tag, scale=True):
        hb = HBT[ts % 2]
        if scale:
            norm_scale(xbuf, ts, kind)
        bk = next_bank()
        pv = bank_bf(bk).rearrange("p (c t) -> p c t", c=8)
        fns = [(lambda e, c=c: e.transpose(pv[:, c, :], hb.ap[:, c * 128:(c + 1) * 128], IDENT.ap)) for c in range(8)]
        P.op("pe", fns, reads=[hb.r(), IDENT.r()], writes=[("ps", bk)], fresh=True, tag=tag)
        gb = {"n1a": CV_GMIX, "n1b": CV_GMIX, "n2": CV_GFFN, "n3": CV_GPG}[kind]
        for c in range(8):
            if c < 4 or not OPT_D:
                P.op("act", (lambda e, c=c: e.activation(out=HT.ap[:, c, ts * 128:(ts + 1) * 128], in_=pv[:, c, :], func=AF.Identity,
                                                         scale=cvcol(gb, c))),
                     reads=[("ps", bk), CV.r()], writes=[(HT.off + c * 1024 + ts * 256, 256)])
            else:
                P.op("dve", (lambda e, c=c: e.tensor_scalar(out=HT.ap[:, c, ts * 128:(ts + 1) * 128], in0=pv[:, c, :],
                                                            scalar1=cvcol(gb, c), scalar2=None, op0=ALU.mult)),
                     reads=[("ps", bk), CV.r()], writes=[(HT.off + c * 1024 + ts * 256, 256)])

    pump_convert(9)
    cst_evs = []

    def cdma(out_ap, in_ap, wr):
        cst_evs.append(P.dma("cst", (lambda e, _o=out_ap, _i=in_ap: e.dma_start(out=_o, in_=_i)), reads=[], writes=[wr]))

    cdma(CV.ap, cv_d[:, :], CV.r())
    cdma(GPLEB.ap, rows_d[2:3, :].partition_broadcast(128), GPLEB.r())
    cdma(GFINB.ap, rows_d[3:4, :].partition_broadcast(128), GFINB.r())
    cdma(arena[:, BSB.off:BSB.off + 4096].bitcast(F32), bs_d[0:1, :].partition_broadcast(128), BSB.r())
    wst_stage = arena[:, GV.off:GV.off + 4096].bitcast(F32).rearrange("p (g t) -> p g t", g=8)
    cdma(wst_stage, wst_d[:, :, :], (GV.off, 4096))
    fin = P.dcnt["cst"]
    for c in list(P.lastw):
        ev = P.lastw[c]
        if ev[0] == "cst":
            P.lastw[c] = ("cst", fin, "dma")
    P.op("pool", (lambda e: e.memset(ONES.ap, 1.0)), reads=[], writes=[ONES.r()])
    P.op("pool", (lambda e: e.memset(ONESB.ap, 1.0)), reads=[], writes=[ONESB.r()])
    P.op("pool", (lambda e: e.memset(MHALF.ap, -0.5)), reads=[], writes=[MHALF.r()])
    P.op("pool", (lambda e: e.memset(EPSB.ap, EPS_LN)), reads=[], writes=[EPSB.r()])
    P.op("pool", (lambda e: e.affine_select(out=IDENT.ap, in_=ONESB.ap, pattern=[[-1, 128]], compare_op=ALU.is_equal,
                                            fill=0.0, base=0, channel_multiplier=1)),
         reads=[ONESB.r()], writes=[IDENT.r()])
    wsm = arena[:, GV.off + 4096:GV.off + 8192].bitcast(F32).rearrange("p (g t) -> p g t", g=8)
    P.op("pool", (lambda e: e.affine_select(out=wsm, in_=wst_stage, pattern=[[0, 8], [1, 128]], compare_op=ALU.is_ge,
                                            fill=0.0, base=0, channel_multiplier=-1)),
         reads=[(GV.off, 4096)], writes=[(GV.off + 4096, 4096)])
    P.op("dve", (lambda e: e.tensor_copy(out=WST.ap, in_=wsm)), reads=[(GV.off + 4096, 4096)], writes=[WST.r()])
    wsm2 = arena[:, GV.off + 4096:GV.off + 8192].bitcast(F32)
    for h in range(2):
        P.op("pe", [lambda e, h=h: e.matmul(bank(h), lhsT=ONES.ap, rhs=wsm2[:, h * 512:(h + 1) * 512], start=True, stop=True)],
             reads=[(GV.off + 4096, 4096), ONES.r()], writes=[("ps", h)])
    for g in range(8):
        P.op("dve", (lambda e, g=g: e.scalar_tensor_tensor(out=BSB.ap[:, g, :], in0=bank(g // 4)[:, (g % 4) * 128:(g % 4 + 1) * 128],
                                                          scalar=cvcol(CV_LNVB, g), in1=BSB.ap[:, g, :], op0=ALU.mult, op1=ALU.add)),
             reads=[("ps", g // 4), BSB.r(g), CV.r()], writes=[BSB.r(g)])
    onesrow = arena[0:1, ONES.off:ONES.off + 512].bitcast(F32)

    out_evs = []

    def load_x(ti):
        xb = XBS[ti % 2]
        P.dma("xl%d" % (ti % 2), (lambda e: e.dma_start(out=xb.ap, in_=x_d[ti * T:(ti + 1) * T, :].rearrange("(ts p) d -> p ts d", p=128))),
              reads=[], writes=[xb.r()])

    def n1kind(ti):
        return "n1a" if ti % 2 == 0 else "n1b"

    def load_p(ti):
        P.dma("pl", (lambda e: e.dma_start(out=PIN.ap, in_=p_d[ti * T:(ti + 1) * T, :].rearrange("(ts p) d -> p ts d", p=128))),
              reads=[], writes=[PIN.r()])
        P.op("dve", (lambda e: e.tensor_copy(out=PBF.ap, in_=PIN.ap)), reads=[PIN.r()], writes=[PBF.r()])

    load_x(0)
    load_p(0)
    for ts in range(4):
        norm_stats(XBS[0], ts, n1kind(0))
    for ts in range(4):
        norm_apply(XBS[0], ts, n1kind(0), "n1T")

    for ti in range(ntile):
        tt = ti % tps
        r0 = ti * T
        XB = XBS[ti % 2]
        if tt == 0:
            P.op("pool", (lambda e: e.memset(GLUT.ap[:, :, 0:30], 0.0)), reads=[], writes=[GLUT.r()])
            P.op("pool", (lambda e: e.memset(HC.ap, 0.0)), reads=[], writes=[HC.r()])
        if ti + 1 < ntile:
            load_x(ti + 1)

        vb = [get_block(), get_block()]

        def v_ts(ts):
            gv = GV.ap[:, ts % 2, :]
            for half in range(2):
                n, wv, wr = vb[half]
                bk = next_bank()
                mm_group(bk, [(HT.ap[:, kc, ts * 128:(ts + 1) * 128], wv[:, kc, :]) for kc in range(8)],
                         reads=ht_ts(ts) + [wr], tag="v")
                P.op("act", (lambda e, gv=gv, half=half, bk=bk: e.activation(out=gv[:, half * 512:(half + 1) * 512],
                                                                             in_=bank(bk), func=AF.Gelu)),
                     reads=[("ps", bk)], writes=[GV.r(ts % 2)])
            bn, mv, lv = BN[ts % 2], MV[ts % 2], LV[ts % 2]
            for half in range(2):
                P.op("dve", (lambda e, gv=gv, half=half, bn=bn: e.bn_stats(out=bn.ap[:, half * 6:(half + 1) * 6],
                                                                          in_=gv[:, half * 512:(half + 1) * 512])),
                     reads=[GV.r(ts % 2)], writes=[bn.r()])
            P.op("dve", (lambda e, bn=bn, mv=mv: e.bn_aggr(out=mv.ap, in_=bn.ap)), reads=[bn.r()], writes=[mv.r()])
            P.op("dve", (lambda e, mv=mv, lv=lv: e.tensor_scalar(out=lv.ap[:, 0:1], in0=mv.ap[:, 1:2], scalar1=EPS_LN,
                                                              scalar2=None, op0=ALU.add)),
                 reads=[mv.r()], writes=[lv.r()])
            P.op("pool", (lambda e, lv=lv: e.tensor_tensor(out=lv.ap[:, 1:2], in0=lv.ap[:, 0:1], in1=MHALF.ap[:, 0:1], op=ALU.pow)),
                 reads=[lv.r(), MHALF.r()], writes=[lv.r()])
            P.op("dve", (lambda e, gv=gv, mv=mv, lv=lv, ts=ts: e.tensor_scalar(out=NBF.ap[:, ts, :], in0=gv, scalar1=mv.ap[:, 0:1],
                                                                            scalar2=lv.ap[:, 1:2], op0=ALU.subtract, op1=ALU.mult)),
                 reads=[GV.r(ts % 2), mv.r(), lv.r()], writes=[NBF.r(ts)])
        for ts in range(4):
            v_ts(ts)
        release(vb[0][0])
        release(vb[1][0])

        bk = next_bank()
        pv = bank_bf(bk).rearrange("p (c t) -> p c t", c=2)
        fns = [(lambda e, ts=ts, kc=kc, pv=pv: e.transpose(pv[:, kc, ts * 128:(ts + 1) * 128], PBF.ap[:, ts, kc * 128:(kc + 1) * 128], IDENT.ap))
               for ts in range(4) for kc in range(2)]
        P.op("pe", fns, reads=[PBF.r(), IDENT.r()], writes=[("ps", bk)], fresh=True, tag="pT")
        P.op("act", (lambda e, pv=pv: e.copy(out=PT.ap, in_=pv)), reads=[("ps", bk)], writes=[PT.r()])

        def build_diag(idx):
            dg = DG[idx % 8]
            P.op("dve", (lambda e: e.tensor_scalar(out=dg.ap, in0=IDENT.ap, scalar1=cvcol(CV_CBW, idx), scalar2=None, op0=ALU.mult)),
                 reads=[IDENT.r(), CV.r()], writes=[dg.r()])

        for blk_i in range(2):
            if blk_i == 1:
                for idx in range(8):
                    build_diag(idx)
            na, wa, wra = get_block()
            ng, wg, wrg = get_block()
            for j in range(4):
                c = blk_i * 4 + j
                bka = next_bank()
                mm_group(bka, [(wa[:, kc, j * 128:(j + 1) * 128], HT.ap[:, kc, :]) for kc in range(8)], reads=[HT.r(), wra], tag="a")
                bkg = next_bank()
                mm_group(bkg, [(wg[:, kc, j * 128:(j + 1) * 128], HT.ap[:, kc, :]) for kc in range(8)], reads=[HT.r(), wrg], tag="gl")
                q = c % 2
                P.op("act", (lambda e, q=q, bkg=bkg: e.activation(out=SQ.ap[:, q, :], in_=bank(bkg), func=AF.Sigmoid)),
                     reads=[("ps", bkg)], writes=[SQ.r(q)])
                P.op("dve", (lambda e, q=q, c=c, bka=bka: e.tensor_tensor(out=GLUT.ap[:, c, 30:542], in0=bank(bka), in1=SQ.ap[:, q, :],
                                                                         op=ALU.mult)),
                     reads=[("ps", bka), SQ.r(q)], writes=[GLUT.r(c)])
            release(na)
            release(ng)

        def lnc_pre(c):
            P.op("dve", (lambda e, c=c: e.tensor_tensor(out=CB.ap[:, c, :], in0=CB.ap[:, c, :], in1=RSTD, op=ALU.mult)),
                 reads=[CB.r(c), STAT.r(2)], writes=[CB.r(c)])
            P.op(("pool" if c % 2 == 0 else "dve"), (lambda e, c=c: e.tensor_tensor(out=CB.ap[:, c, :], in0=CB.ap[:, c, :], in1=MEAN, op=ALU.add)),
                 reads=[CB.r(c), STAT.r(0)], writes=[CB.r(c)])

        def lnc_silu(c):
            P.op("act", (lambda e, c=c: e.activation(out=SCT.ap[:, c, :], in_=CB.ap[:, c, :], func=AF.Silu,
                                                     scale=cvcol(CV_LNBG, c), bias=cvcol(CV_LNBB, c))),
                 reads=[CB.r(c), CV.r()], writes=[SCT.r(c)])

        pend_stats = None

        def stats_mm(c, first, last):
            P.op("pe", [lambda e, c=c, first=first, last=last: e.matmul(bank(6), lhsT=ONES.ap, rhs=CB.ap[:, c, :], start=first, stop=last)],
                 reads=[CB.r(c), ONES.r()], writes=[("ps", 6)], tag="stats")
            P.op("pe", [lambda e, c=c, first=first, last=last: e.matmul(bank(7), lhsT=ONES.ap, rhs=SQ.ap[:, c % 2, :], start=first, stop=last)],
                 reads=[SQ.r(c % 2), ONES.r()], writes=[("ps", 7)])
        for c in range(8):
            bk = next_bank()
            for k in range(31):
                dg = DG[(c * 31 + k) % 8]
                if c * 31 + k >= 8:
                    build_diag(c * 31 + k)
                P.op("pe", [lambda e, dg=dg, c=c, k=k, bk=bk: e.matmul(bank(bk), lhsT=dg.ap, rhs=GLUT.ap[:, c, k:k + 512],
                                                                     start=(k == 0), stop=(k == 30))],
                     reads=[dg.r(), GLUT.r(c)], writes=[("ps", bk)], tag=("conv" if k == 0 else None))
            if pend_stats is not None:
                stats_mm(pend_stats, pend_stats == 0, False)
            P.op("act", (lambda e, c=c, bk=bk: e.activation(out=CB.ap[:, c, :], in_=bank(bk), func=AF.Identity, bias=cvcol(CV_CBB, c))),
                 reads=[("ps", bk), CV.r()], writes=[CB.r(c)])
            P.op("act", (lambda e, c=c: e.activation(out=SQ.ap[:, c % 2, :], in_=CB.ap[:, c, :], func=AF.Square)),
                 reads=[CB.r(c)], writes=[SQ.r(c % 2)])
            pend_stats = c
        stats_mm(7, False, True)
        P.op("pool", (lambda e: e.tensor_copy(out=GLUT.ap[:, :, 0:30], in_=GLUT.ap[:, :, 512:542])), reads=[GLUT.r()], writes=[GLUT.r()])
        MEAN, VAR, RSTD = STAT.ap[:, 0, :], STAT.ap[:, 1, :], STAT.ap[:, 2, :]
        P.op("dve", (lambda e: e.tensor_scalar(out=MEAN, in0=bank(6), scalar1=1.0 / D, scalar2=None, op0=ALU.mult)),
             reads=[("ps", 6)], writes=[STAT.r(0)])
        if OPT_SQ:
            P.op("act", (lambda e: e.activation(out=VAR, in_=bank(6), func=AF.Square, scale=1.0 / D)), reads=[("ps", 6)], writes=[STAT.r(1)])
        else:
            P.op("dve", (lambda e: e.tensor_tensor(out=VAR, in0=MEAN, in1=MEAN, op=ALU.mult)), reads=[STAT.r(0)], writes=[STAT.r(1)])
        P.op("dve", (lambda e: e.scalar_tensor_tensor(out=VAR, in0=bank(7), scalar=1.0 / D, in1=VAR, op0=ALU.mult, op1=ALU.subtract)),
             reads=[("ps", 7), STAT.r(1)], writes=[STAT.r(1)])
        P.op("act", (lambda e: e.activation(out=VAR, in_=VAR, func=AF.Sqrt, bias=EPSB.ap, scale=1.0)),
             reads=[STAT.r(1), EPSB.r()], writes=[STAT.r(1)])
        P.op("dve", (lambda e: e.reciprocal(out=RSTD, in_=VAR)), reads=[STAT.r(1)], writes=[STAT.r(2)])
        P.op("dve", (lambda e: e.scalar_tensor_tensor(out=MEAN, in0=MEAN, scalar=-1.0, in1=RSTD, op0=ALU.mult, op1=ALU.mult)),
             reads=[STAT.r(0), STAT.r(2)], writes=[STAT.r(0)])

        def mix_g(g):
            bk = next_bank()
            fns = []
            for ts in range(4):
                o = bank(bk)[:, ts * 128:(ts + 1) * 128]
                fns.append(lambda e, o=o, g=g, ts=ts: e.matmul(o, lhsT=NBF.ap[:, ts, g * 128:(g + 1) * 128], rhs=WST.ap[:, g, :],
                                                              start=True, stop=True))
            P.op("pe", fns, reads=[NBF.r(), WST.r()], writes=[("ps", bk)], fresh=True, tag="mix")
            q = g % 2
            bv = bank(bk).rearrange("p (a t) -> p a t", a=4)
            tv = SQ.ap[:, q, :].rearrange("p (a t) -> p a t", a=4)
            bb = BSB.ap[:, g:g + 1, :].broadcast_to([128, 4, 128])
            P.op("dve", (lambda e, bv=bv, tv=tv, bb=bb, g=g: e.scalar_tensor_tensor(out=tv, in0=bv, scalar=cvcol(CV_LNVG, g), in1=bb,
                                                                                   op0=ALU.mult, op1=ALU.add)),
                 reads=[("ps", bk), BSB.r(g), CV.r()], writes=[SQ.r(q)])
            P.op("dve", (lambda e, g=g, q=q: e.tensor_tensor(out=UMT.ap[:, g, :], in0=SQ.ap[:, q, :], in1=GUT.ap[:, g, :], op=ALU.mult)),
                 reads=[SQ.r(q), GUT.r(g)], writes=[UMT.r(g)])

        def u_blk(blk_i, with_mix):
            n, wv, wr = get_block()
            for j in range(4):
                bk = next_bank()
                mm_group(bk, [(wv[:, kc, j * 128:(j + 1) * 128], HT.ap[:, kc, :]) for kc in range(8)], reads=[HT.r(), wr], tag="u")
                c = blk_i * 4 + j
                P.op("act", (lambda e, c=c, bk=bk: e.activation(out=GUT.ap[:, c, :], in_=bank(bk), func=AF.Gelu)),
                     reads=[("ps", bk)], writes=[GUT.r(c)])
                if with_mix:
                    mix_g(j)
                    if j >= 1:
                        mix_g(4 + j - 1)
            return n

        for c in range(4):
            lnc_pre(c)
        release(u_blk(0, False))
        release(u_blk(1, True))
        mix_g(7)
        for c in range(4, 8):
            lnc_pre(c)

        def out_branch(ACT_T, is_b):
            for blk_i in range(2):
                if blk_i == 1 and not is_b:
                    for c in range(8):
                        lnc_silu(c)
                ngt, wgt, wrgt = get_block()
                nw, ww, wrw = get_block()

                def gate(j):
                    bg = next_bank()
                    mm_group(bg, [(wgt[:, kc, j * 128:(j + 1) * 128], HT.ap[:, kc, :]) for kc in range(8)], reads=[HT.r(), wrgt], tag="gate")
                    return bg

                bgs = {0: gate(0), 1: gate(1), 2: gate(2)}
                for j in range(4):
                    m = blk_i * 4 + j
                    bg = bgs[j]
                    by = next_bank()
                    if m == 0:
                        mm_group(by, [(ww[:, kc, j * 128:(j + 1) * 128], ACT_T.ap[:, kc, :]) for kc in range(6)], reads=[ACT_T.r(0, 6), wrw],
                                 first=True, last=False, tag="yout")
                        mm_group(by, [(ww[:, kc, j * 128:(j + 1) * 128], ACT_T.ap[:, kc, :]) for kc in range(6, 8)], reads=[ACT_T.r(6, 2), wrw],
                                 first=False, last=True, tag="yout")
                    else:
                        mm_group(by, [(ww[:, kc, j * 128:(j + 1) * 128], ACT_T.ap[:, kc, :]) for kc in range(8)], reads=[ACT_T.r(), wrw], tag="yout")
                    if j + 3 < 4:
                        bgs[j + 3] = gate(j + 3)
                    q = m % 2
                    P.op("act", (lambda e, q=q, bg=bg: e.activation(out=SG.ap[:, q, :], in_=bank(bg), func=AF.Sigmoid)),
                         reads=[("ps", bg)], writes=[SG.r(q)])
                    if not is_b:
                        P.op("dve", (lambda e, q=q, m=m, by=by: e.tensor_tensor(out=T1.ap[:, m, :], in0=bank(by), in1=SG.ap[:, q, :], op=ALU.mult)),
                             reads=[("ps", by), SG.r(q)], writes=[T1.r(m)])
                    else:
                        P.op("dve", (lambda e, q=q, m=m, by=by: e.tensor_tensor(out=STAT.ap[:, q, :], in0=bank(by), in1=SG.ap[:, q, :], op=ALU.mult)),
                             reads=[("ps", by), SG.r(q)], writes=[STAT.r(q)])
                        P.op("dve", (lambda e, q=q, m=m: e.tensor_tensor(out=MGT.ap[:, m, :], in0=T1.ap[:, m, :], in1=STAT.ap[:, q, :], op=ALU.add)),
                             reads=[T1.r(m), STAT.r(q)], writes=[MGT.r(m)])
                release(ngt)
                release(nw)

        out_branch(UMT, False)

        out_branch(SCT, True)

        wo = [get_block(), get_block()]
        for ts in range(4):
            if ts >= 1:
                norm_scale(XB, ts - 1, "n2")
            wbk = {}
            if ts == 0:
                for half in range(2):
                    n, wv, wr = wo[half]
                    wbk[half] = next_bank()
                    mm_group(wbk[half], [(MGT.ap[:, kc, 0:128], wv[:, kc, :]) for kc in range(6)], reads=[MGT.r(0, 6), wr],
                             first=True, last=False, tag="wo")
            for half in range(2):
                n, wv, wr = wo[half]
                if ts == 0:
                    bk = wbk[half]
                    mm_group(bk, [(MGT.ap[:, kc, 0:128], wv[:, kc, :]) for kc in range(6, 8)], reads=[MGT.r(6, 2), wr],
                             first=False, last=True, tag="wo")
                else:
                    bk = next_bank()
                    mm_group(bk, [(MGT.ap[:, kc, ts * 128:(ts + 1) * 128], wv[:, kc, :]) for kc in range(8)], reads=[MGT.r(), wr], tag="wo")
                xs = XB.ap[:, ts, half * 512:(half + 1) * 512]
                P.op("dve", (lambda e, xs=xs, bk=bk: e.tensor_tensor(out=xs, in0=xs, in1=bank(bk), op=ALU.add)),
                     reads=[("ps", bk), XB.r(ts)], writes=[XB.r(ts)])
            norm_stats(XB, ts, "n2")
            if ts >= 2:
                norm_apply(XB, ts - 2, "n2", "n2T", scale=False)
        release(wo[0][0])
        release(wo[1][0])

        norm_scale(XB, 3, "n2")
        norm_apply(XB, 2, "n2", "n2T", scale=False)
        n, wv, wr = get_block()

        def ple_mm(ts, half):
            bk = next_bank()
            mm_group(bk, [(PT.ap[:, kc, ts * 128:(ts + 1) * 128], wv[:, kc, half * 512:(half + 1) * 512]) for kc in range(2)],
                     reads=[PT.r(), wr], tag="ple")
            return bk

        def ple_copy(ts, half, bk):
            P.op("act", (lambda e: e.copy(out=PEN.ap[:, ts, half * 512:(half + 1) * 512], in_=bank(bk))),
                 reads=[("ps", bk)], writes=[PEN.r(ts)])

        early = [(ts, half, ple_mm(ts, half)) for ts in range(2) for half in range(2)]
        norm_apply(XB, 3, "n2", "n2T", scale=False)
        for (ts, half, bk) in early:
            ple_copy(ts, half, bk)
        for ts in range(2, 4):
            for half in range(2):
                ple_copy(ts, half, ple_mm(ts, half))
        release(n)
        if ti + 1 < ntile:
            load_p(ti + 1)

        for i in range(11):
            n, wv, wr = get_block()
            for q in range(2):
                j = 2 * i + q
                accs = []
                for which in range(2):
                    jj = j + which * NFC
                    bk = next_bank()
                    co = which * 256 + q * 128
                    mm_group(bk, [(wv[:, kc, co:co + 128], HT.ap[:, kc, :]) for kc in range(8)], reads=[HT.r(), wr], tag="up")
                    a = (q * 2 + which)
                    acc = ACC.ap[:, a, :]
                    accs.append(acc)
                    w0, w1, w2 = (cvcol(CV_FCW, k * 44 + jj) for k in range(3))
                    P.op("act", (lambda e, acc=acc, bk=bk, w2=w2, jj=jj: e.activation(out=acc, in_=bank(bk), func=AF.Identity,
                                                                                   scale=w2, bias=cvcol(CV_FCB, jj))),
                         reads=[("ps", bk), CV.r()], writes=[ACC.r(a)])
                    P.op("dve", (lambda e, acc=acc, bk=bk, w1=w1: e.scalar_tensor_tensor(out=acc[:, 1:512], in0=bank(bk)[:, 0:511], scalar=w1,
                                                                                       in1=acc[:, 1:512], op0=ALU.mult, op1=ALU.add)),
                         reads=[("ps", bk), ACC.r(a), CV.r()], writes=[ACC.r(a)])
                    P.op("dve", (lambda e, acc=acc, bk=bk, w0=w0: e.scalar_tensor_tensor(out=acc[:, 2:512], in0=bank(bk)[:, 0:510], scalar=w0,
                                                                                       in1=acc[:, 2:512], op0=ALU.mult, op1=ALU.add)),
                         reads=[("ps", bk), ACC.r(a), CV.r()], writes=[ACC.r(a)])
                    P.op(("pool" if OPT_TINY else "dve"), (lambda e, acc=acc, jj=jj: e.tensor_tensor(out=acc[:, 0:2], in0=acc[:, 0:2], in1=HC.ap[:, :, jj], op=ALU.add)),
                         reads=[ACC.r(a), HC.r()], writes=[ACC.r(a)])
                    P.op("dve", (lambda e, bk=bk, jj=jj: e.tensor_copy(out=HAL.ap[:, :, jj], in_=bank(bk)[:, 510:512])),
                         reads=[("ps", bk)], writes=[HAL.r()])
                P.op("act", (lambda e, q=q, acc=accs[0]: e.activation(out=GEL.ap[:, q, :], in_=acc, func=AF.Gelu)),
                     reads=[ACC.r(q * 2)], writes=[GEL.r(q)])
                P.op("pool", (lambda e, q=q, j=j, acc=accs[1]: e.tensor_tensor(out=HMT.ap[:, j, :], in0=GEL.ap[:, q, :], in1=acc, op=ALU.mult)),
                     reads=[GEL.r(q), ACC.r(q * 2 + 1)], writes=[HMT.r(j)])
            release(n)
        FW0, FW1 = CV.ap[:, CV_FCW:CV_FCW + 44], CV.ap[:, CV_FCW + 44:CV_FCW + 88]
        P.op("dve", (lambda e: e.tensor_tensor(out=HC.ap[:, 0, :], in0=HAL.ap[:, 1, :], in1=FW1, op=ALU.mult)),
             reads=[HAL.r(), CV.r()], writes=[HC.r()])
        P.op("dve", (lambda e: e.tensor_tensor(out=TMPH.ap, in0=HAL.ap[:, 0, :], in1=FW0, op=ALU.mult)),
             reads=[HAL.r(), CV.r()], writes=[TMPH.r()])
        P.op("dve", (lambda e: e.tensor_tensor(out=HC.ap[:, 0, :], in0=HC.ap[:, 0, :], in1=TMPH.ap, op=ALU.add)),
             reads=[HC.r(), TMPH.r()], writes=[HC.r()])
        P.op("dve", (lambda e: e.tensor_tensor(out=HC.ap[:, 1, :], in0=HAL.ap[:, 1, :], in1=FW0, op=ALU.mult)),
             reads=[HAL.r(), CV.r()], writes=[HC.r()])

        for ts in range(4):
            P.op("dve", (lambda e, ts=ts: e.scalar_tensor_tensor(out=JUNKF.ap, in0=PEN.ap[:, ts, :], scalar=1.0, in1=PEN.ap[:, ts, :],
                                                                op0=ALU.mult, op1=ALU.mult, accum_out=SP4.ap[:, ts:ts + 1])),
                 reads=[PEN.r(ts)], writes=[JUNKF.r(), SP4.r()])
        P.op("dve", (lambda e: e.tensor_scalar(out=SP4.ap, in0=SP4.ap, scalar1=1.0 / D, scalar2=EPS_RMS, op0=ALU.mult, op1=ALU.add)),
             reads=[SP4.r()], writes=[SP4.r()])
        P.op("pool", (lambda e: e.tensor_tensor(out=SP4.ap, in0=SP4.ap, in1=MHALF.ap[:, 0:4], op=ALU.pow)),
             reads=[SP4.r(), MHALF.r()], writes=[SP4.r()])
        for ts in range(4):
            P.op("dve", (lambda e, ts=ts: e.scalar_tensor_tensor(out=PEN.ap[:, ts, :], in0=PEN.ap[:, ts, :], scalar=SP4.ap[:, ts:ts + 1],
                                                                in1=GPLEB.ap, op0=ALU.mult, op1=ALU.mult)),
                 reads=[PEN.r(ts), SP4.r(), GPLEB.r()], writes=[PEN.r(ts)])

        if ti + 1 < ntile:
            for ts in range(4):
                norm_stats(XBS[(ti + 1) % 2], ts, n1kind(ti + 1))
        for tsg in range(2):
            tss = (2 * tsg, 2 * tsg + 1)
            for half in range(2):
                if tsg == 0 and half == 0:
                    bks = {tss[0]: 6, tss[1]: 7}
                else:
                    bks = {ts: next_bank() for ts in tss}
                for (k0, nk) in ((0, 8), (8, 8), (16, 6)):
                    n, wv, wr = get_block()
                    for ts in tss:
                        mm_group(bks[ts], [(HMT.ap[:, k0 + kc, ts * 128:(ts + 1) * 128], wv[:, kc, :]) for kc in range(nk)],
                                 reads=[HMT.r(k0, nk), wr], first=(k0 == 0), last=(k0 == 16), tag="down")
                        if k0 == 16:
                            xs = XB.ap[:, ts, half * 512:(half + 1) * 512]
                            P.op("dve", (lambda e, xs=xs, bk=bks[ts]: e.tensor_tensor(out=xs, in0=xs, in1=bank(bk), op=ALU.add)),
                                 reads=[("ps", bks[ts]), XB.r(ts)], writes=[XB.r(ts)])
                            if half == 1:
                                norm_stats(XB, ts, "n3")
                    release(n)
                if tsg == 1 and half == 0:
                    norm_apply(XB, 0, "n3", "n3T", scale=False)
                    norm_apply(XB, 1, "n3", "n3T", scale=False)
            if tsg == 0:
                norm_scale(XB, 0, "n3")
                norm_scale(XB, 1, "n3")
        wpg = [get_block(), get_block()]

        def pg_ts(ts):
            for half in range(2):
                n, wv, wr = wpg[half]
                bk = next_bank()
                mm_group(bk, [(HT.ap[:, kc, ts * 128:(ts + 1) * 128], wv[:, kc, :]) for kc in range(8)], reads=ht_ts(ts) + [wr], tag="pg")
                q = (ts * 2 + half) % 2
                P.op("act", (lambda e, q=q, bk=bk: e.activation(out=SGP.ap[:, q, :], in_=bank(bk), func=AF.Sigmoid)),
                     reads=[("ps", bk)], writes=[SGP.r(q)])
                P.op("dve", (lambda e, q=q, ts=ts, half=half: e.tensor_tensor(out=SGP.ap[:, q, :], in0=SGP.ap[:, q, :],
                                                                             in1=PEN.ap[:, ts, half * 512:(half + 1) * 512], op=ALU.mult)),
                     reads=[SGP.r(q), PEN.r(ts)], writes=[SGP.r(q)])
                xs = XB.ap[:, ts, half * 512:(half + 1) * 512]
                P.op("dve", (lambda e, xs=xs, q=q: e.tensor_tensor(out=xs, in0=xs, in1=SGP.ap[:, q, :], op=ALU.add)),
                     reads=[SGP.r(q), XB.r(ts)], writes=[XB.r(ts)])

        norm_scale(XB, 2, "n3")
        norm_scale(XB, 3, "n3")
        pg_ts(0)
        norm_apply(XB, 2, "n3", "n3T", scale=False)
        pg_ts(1)
        norm_apply(XB, 3, "n3", "n3T", scale=False)
        nxt = ti + 1 < ntile
        XN = XBS[(ti + 1) % 2]
        nk1 = n1kind(ti + 1)
        if nxt:
            norm_scale(XN, 0, nk1)
            norm_scale(XN, 1, nk1)
        pg_ts(2)
        if nxt:
            norm_apply(XN, 0, nk1, "n1T", scale=False)
            norm_apply(XN, 1, nk1, "n1T", scale=False)
            norm_scale(XN, 2, nk1)
            norm_scale(XN, 3, nk1)
        pg_ts(3)
        release(wpg[0][0])
        release(wpg[1][0])
        if nxt:
            norm_apply(XN, 2, nk1, "n1T", scale=False)
            norm_apply(XN, 3, nk1, "n1T", scale=False)

        for ts in range(4):
            norm_stats(XB, ts, "fin")
        for ts in range(4):
            rs = NSC["fin"][ts][2]
            P.op("dve", (lambda e, ts=ts, rs=rs, XB=XB: e.scalar_tensor_tensor(out=OUTB.ap[:, ts, :], in0=XB.ap[:, ts, :], scalar=rs.ap,
                                                                       in1=GFINB.ap, op0=ALU.mult, op1=ALU.mult)),
                 reads=[XB.r(ts), rs.r(), GFINB.r()], writes=[OUTB.r(ts)])
        out_evs.append(P.dma("os", (lambda e, r0=r0: e.dma_start(out=out_d[r0:r0 + T, :].rearrange("(ts p) d -> p ts d", p=128), in_=OUTB.ap)),
                             reads=[OUTB.r()], writes=[]))

    fin_evs = [(s, v, "dma") for s, v in P.dcnt.items()]
    P.wait_all("sp", fin_evs)

    build_nc.last_prog = P
    import contextlib
    with contextlib.ExitStack() as es:
        for nme in sem_names:
            sems[nme] = es.enter_context(nc.semaphore("s_" + nme))
        block = es.enter_context(nc.Block())

        def replay(e, key):
            for waits, fns, inc, tag in P.ops[key]:
                for sem, val in waits:
                    e.wait_ge(sems[sem], val)
                for i, fn in enumerate(fns):
                    _check_snap(fn)
                    ins = fn(e)
                    if key == "pe":
                        P.pe_names.append((ins.ins.name, tag))
                    if tag is not None and i == 0:
                        ins.annotate(tag)
                    if inc is not None and i == len(fns) - 1:
                        ins.then_inc(sems[inc[0]], inc[1])

        @block.sync
        def _(e):
            replay(e, "sp")

        @block.tensor
        def _(e):
            replay(e, "pe")

        @block.scalar
        def _(e):
            replay(e, "act")

        @block.vector
        def _(e):
            replay(e, "dve")

        @block.gpsimd
        def _(e):
            replay(e, "pool")
    return nc


def prep_shared(inp):
    f = lambda a: np.ascontiguousarray(np.asarray(a, dtype=np.float32))
    col8 = lambda v: f(v).reshape(8, 128).T
    cv = np.concatenate([
        col8(inp["g_mix"][0]), col8(inp["g_ffn"][0]), col8(inp["g_pg"][0]),
        col8(inp["ln_b_g"][0]), col8(inp["ln_b_b"][0]), col8(inp["conv_b_b"][0]),
        f(inp["conv_b_w"][0]).reshape(31, 8, 128).transpose(2, 1, 0).reshape(128, 8 * 31),
        f(inp["ffn_conv_w"][0]).reshape(3, 44, 128).transpose(2, 0, 1).reshape(128, 3 * 44),
        f(inp["ffn_conv_b"][0]).reshape(44, 128).T,
        col8(inp["ln_v_g"][0]), col8(inp["ln_v_b"][0]),
    ], axis=1)
    assert cv.shape == (128, NCV)
    rows = np.stack([f(inp["ln_v_g"][0]), f(inp["ln_v_b"][0]), f(inp["g_ple"][0]), f(inp["g_final"])], axis=0)
    return {
        "w_in": f(inp["w_in"][0]), "w_a_out": f(inp["w_a_out"][0]), "w_b_out": f(inp["w_b_out"][0]),
        "w_o": f(inp["w_o"][0]), "w_up": f(inp["w_up"][0]), "w_down": f(inp["w_down"][0]),
        "w_pg": f(inp["w_pg"][0]), "w_ple": f(inp["w_ple"][0]),
        "cv": f(cv), "rows": f(rows), "bs": f(inp["b_s"][0]).reshape(1, 1024),
        "wst": f(np.transpose(f(inp["w_s"][0]), (2, 0, 1))),
    }


def kernel(**inputs):
    x = np.asarray(inputs["x"], dtype=np.float32)
    p = np.asarray(inputs["p"], dtype=np.float32)[0]
    bsz, seq, _ = x.shape
    ncores = 8
    n_seq = bsz // ncores
    tps = seq // T
    nc = build_nc(n_seq, tps)
    shared = prep_shared(inputs)
    in_maps = []
    for c in range(ncores):
        m = dict(shared)
        m["x"] = np.ascontiguousarray(x[c * n_seq:(c + 1) * n_seq].reshape(n_seq * seq, D))
        m["p"] = np.ascontiguousarray(p[c * n_seq:(c + 1) * n_seq].reshape(n_seq * seq, 256))
        in_maps.append(m)
    res = run_bass_kernel_spmd(nc, in_maps, core_ids=list(range(ncores)))
    out = np.concatenate([r["out"].reshape(n_seq, seq, D) for r in res.results], axis=0)
    return out.astype(np.float32)
```

```python
import numpy as np
import concourse.bass as bass
import concourse.mybir as mybir
from concourse.bass_utils import run_bass_kernel_spmd

F32 = mybir.dt.float32
BF16 = mybir.dt.bfloat16
U8 = mybir.dt.uint8
AF = mybir.ActivationFunctionType
ALU = mybir.AluOpType

D = 1024
FF = 2816
NFC = 22
T = 512
EPS_RMS = 1e-6
EPS_LN = 1e-5
SAME_ENGINE_SYNC = True
OPT_TINY = True
OPT_CAST = True
OPT_SQ = False
OPT_D = False
NSLOT = 5
NSTG = 2

CV_GMIX, CV_GFFN, CV_GPG, CV_LNBG, CV_LNBB, CV_CBB = 0, 8, 16, 24, 32, 40
CV_CBW = 48
CV_FCW = CV_CBW + 8 * 31
CV_FCB = CV_FCW + 3 * 44
CV_LNVG = CV_FCB + 44
CV_LNVB = CV_LNVG + 8
NCV = CV_LNVB + 8


class Buf:
    def __init__(self, arena, off, shape, dt, esz, parts=128):
        self.off, self.shape, self.esz = off, tuple(shape), esz
        n = int(np.prod(shape))
        self.nbytes = n * esz
        ap = arena[0:parts, off:off + n * esz].bitcast(dt)
        if len(shape) == 2:
            ap = ap.rearrange("p (a b) -> p a b", a=shape[0])
        elif len(shape) == 3:
            ap = ap.rearrange("p (a b c) -> p a b c", a=shape[0], b=shape[1])
        self.ap = ap
        self.slot_bytes = self.nbytes // shape[0] if len(shape) >= 2 else self.nbytes

    def __getitem__(self, k):
        return self.ap[k]

    def r(self, i=None, n=1):
        if i is None:
            return (self.off, self.nbytes)
        return (self.off + i * self.slot_bytes, n * self.slot_bytes)


def _snap(fn):
    cl = fn.__closure__ or ()
    fn._snap = [c.cell_contents for c in cl]
    return fn


def _check_snap(fn):
    cl = fn.__closure__ or ()
    for c, v in zip(cl, fn._snap):
        w = c.cell_contents
        assert w is v or (type(v) in (int, float, str, bool) and w == v) or repr(w) == repr(v), ("late-bound closure variable changed", fn.__code__.co_firstlineno)


class Prog:
    def __init__(self, small_base):
        self.ops = {e: [] for e in ("pe", "act", "dve", "pool", "sp")}
        self.cnt = {e: 0 for e in ("pe", "act", "dve", "pool")}
        self.waited = {e: {} for e in self.ops}
        self.lastw = {}
        self.readers = {}
        self.dcnt = {}
        self.small_base = small_base
        self.pe_names = []

    def cells(self, rng):
        if isinstance(rng, tuple) and isinstance(rng[0], str):
            return [rng]
        off, n = rng
        g = 32 if off >= self.small_base else 256
        return [("sb", g, c) for c in range(off // g, (off + n - 1) // g + 1)]

    def _deps(self, eng, reads, writes):
        need = {}

        def add(ev):
            if ev is None:
                return
            sem, val, src = ev
            if src == eng and (eng == "pe" or not SAME_ENGINE_SYNC):
                return
            if need.get(sem, 0) < val:
                need[sem] = val
        for r in reads:
            for c in self.cells(r):
                add(self.lastw.get(c))
        for w in writes:
            for c in self.cells(w):
                add(self.lastw.get(c))
                for ev in self.readers.get(c, ()):
                    add(ev)
        out = []
        wd = self.waited[eng]
        for sem, val in need.items():
            if wd.get(sem, 0) < val:
                wd[sem] = val
                out.append((sem, val))
        return out

    def _commit(self, ev, reads, writes):
        for r in reads:
            for c in self.cells(r):
                self.readers.setdefault(c, []).append(ev)
        for w in writes:
            for c in self.cells(w):
                self.lastw[c] = ev
                self.readers[c] = []

    def op(self, eng, fns, reads, writes, fresh=False, tag=None):
        if not isinstance(fns, (list, tuple)):
            fns = [fns]
        if fresh:
            for w in writes:
                for c in self.cells(w):
                    assert c not in self.lastw or self.readers.get(c), ("overwriting unread data", c)
        waits = self._deps(eng, reads, writes)
        self.cnt[eng] += 1
        ev = (eng, self.cnt[eng], eng)
        self.ops[eng].append((waits, [_snap(f) for f in fns], (eng, 1), tag))
        self._commit(ev, reads, writes)

    def dma(self, sem, fn, reads, writes, eng="sp"):
        waits = self._deps(eng, reads, writes)
        self.dcnt[sem] = self.dcnt.get(sem, 0) + 16
        ev = (sem, self.dcnt[sem], "dma")
        self.ops[eng].append((waits, [_snap(fn)], (sem, 16), None))
        self._commit(ev, reads, writes)
        return ev

    def wait_all(self, eng, evs):
        waits = []
        wd = self.waited[eng]
        for sem, val, _ in evs:
            if wd.get(sem, 0) < val:
                wd[sem] = val
                waits.append((sem, val))
        self.ops[eng].append((waits, [], None, None))


def build_nc(n_seq, tps):
    ntile = n_seq * tps
    ntok = ntile * T
    nc = bass.Bass("TRN2", target_bir_lowering=False)

    x_d = nc.dram_tensor("x", [ntok, D], F32, kind="ExternalInput").ap()
    p_d = nc.dram_tensor("p", [ntok, 256], F32, kind="ExternalInput").ap()
    w_in_d = nc.dram_tensor("w_in", [D, 6144], F32, kind="ExternalInput").ap()
    w_a_d = nc.dram_tensor("w_a_out", [D, D], F32, kind="ExternalInput").ap()
    w_b_d = nc.dram_tensor("w_b_out", [D, D], F32, kind="ExternalInput").ap()
    w_o_d = nc.dram_tensor("w_o", [D, D], F32, kind="ExternalInput").ap()
    w_up_d = nc.dram_tensor("w_up", [D, 2 * FF], F32, kind="ExternalInput").ap()
    w_dn_d = nc.dram_tensor("w_down", [FF, D], F32, kind="ExternalInput").ap()
    w_pg_d = nc.dram_tensor("w_pg", [D, D], F32, kind="ExternalInput").ap()
    w_ple_d = nc.dram_tensor("w_ple", [256, D], F32, kind="ExternalInput").ap()
    cv_d = nc.dram_tensor("cv", [128, NCV], F32, kind="ExternalInput").ap()
    rows_d = nc.dram_tensor("rows", [4, D], F32, kind="ExternalInput").ap()
    bs_d = nc.dram_tensor("bs", [1, 1024], F32, kind="ExternalInput").ap()
    wst_d = nc.dram_tensor("wst", [128, 8, 128], F32, kind="ExternalInput").ap()
    out_d = nc.dram_tensor("out", [ntok, D], F32, kind="ExternalOutput").ap()
    NBLK = 44
    scr_d = nc.dram_tensor("wscr", [NBLK, 128, 4096], BF16, kind="Internal").ap()

    total = nc.sbuf_bytes_remaining - 64
    total -= total % 64
    arena = nc.alloc_sbuf_tensor("arena", [128, total], U8)
    cur = [0]

    def alloc(nbytes):
        o = cur[0]
        cur[0] += (nbytes + 511) // 512 * 512
        return o

    def buf(shape, dt, off=None):
        esz = 4 if dt == F32 else 2
        n = int(np.prod(shape)) * esz
        if off is None:
            off = alloc(n)
        return Buf(arena, off, shape, dt, esz)

    XBS = [buf([4, 1024], F32) for _ in range(2)]
    HT = buf([8, 512], BF16)
    WS = [buf([4096], BF16) for _ in range(NSLOT)]
    a1 = alloc(24576)
    GUT = buf([8, 512], BF16, a1)
    UMT = buf([8, 512], BF16, a1 + 8192)
    NBF = buf([4, 1024], BF16, a1 + 16384)
    HMT = buf([22, 512], BF16, a1)
    MGT = buf([8, 512], BF16, a1)
    GLUT = buf([8, 542], BF16)
    a2 = alloc(20480)
    CB = buf([8, 512], F32, a2)
    SQ = buf([2, 512], F32, a2 + 16384)
    PEN = buf([4, 1024], F32, a2)
    SG = buf([2, 512], F32, a2 + 16384)
    STAT = buf([3, 512], F32)
    SCT = buf([8, 512], BF16)
    a3 = alloc(16384)
    T1 = buf([8, 512], F32, a3)
    GV = buf([2, 1024], F32, a3)
    ACC = buf([4, 512], F32, a3)
    GEL = buf([2, 512], F32, a3 + 8192)
    SGP = buf([2, 512], F32, a3 + 12288)
    OUTB = buf([4, 1024], F32, a2)
    HBT = [buf([1024], BF16) for _ in range(2)]
    JUNK = buf([1024], BF16)
    JUNKF = buf([1024], F32)
    PIN = buf([4, 256], F32)
    PBF = buf([4, 256], BF16)
    PT = buf([2, 512], BF16)
    GPLEB = buf([1024], F32)
    GFINB = buf([1024], F32)
    WST = buf([8, 128], BF16)
    IDENT = buf([128], BF16)
    ONESB = buf([128], BF16)
    ONES = buf([128], F32)
    DG = [buf([128], BF16) for _ in range(8)]
    BSB = buf([8, 128], F32)
    small_base = cur[0]
    cur[0] = small_base

    def sbuf_small(shape, dt=F32):
        esz = 4 if dt == F32 else 2
        n = int(np.prod(shape)) * esz
        o = cur[0]
        cur[0] += (n + 31) // 32 * 32
        return Buf(arena, o, shape, dt, esz)

    CV = sbuf_small([NCV])
    MHALF = sbuf_small([8])
    EPSB = sbuf_small([1])
    NSC = {k: [[sbuf_small([1]) for _ in range(3)] for _ in range(4)] for k in ("n1a", "n1b", "n2", "n3", "fin")}
    BN = [sbuf_small([12]) for _ in range(2)]
    MV = [sbuf_small([2]) for _ in range(2)]
    LV = [sbuf_small([2]) for _ in range(2)]
    HAL = sbuf_small([2, 44])
    HC = sbuf_small([2, 44])
    TMPH = sbuf_small([44])
    SSP = sbuf_small([8])
    SP4 = sbuf_small([4])
    assert cur[0] <= total, (cur[0], total)

    ps = nc.alloc_psum_tensor("ps", [128, 4096], F32)

    def bank(b):
        return ps[:, b * 512:(b + 1) * 512]

    def bank_bf(b):
        return ps[:, b * 512:(b + 1) * 512].bitcast(BF16)

    P = Prog(small_base)
    bank_rr = [0]

    def next_bank():
        b = bank_rr[0]
        bank_rr[0] = (b + 1) % 6
        return b

    def cvcol(base, j):
        return CV.ap[:, base + j:base + j + 1]

    def wsrc(w_ap, c0, ncol):
        v = w_ap.rearrange("(kc p) n -> p kc n", p=128)
        return lambda kc0, nkc, _v=v, _c0=c0, _n=ncol: _v[:, kc0:kc0 + nkc, _c0:_c0 + _n]

    blocks = []

    def add_simple(w_ap, c0, gbase=None, nk=8, k0=0):
        src = wsrc(w_ap, c0, 512)
        pieces = []
        kc = 0
        while kc < nk:
            n = min(2, nk - kc)
            pieces.append((kc, n, 0, 512, (lambda a, b, _s=src, _k0=k0: _s(a + _k0, b))))
            kc += n
        blocks.append(dict(nk=nk, ncol=512, pieces=pieces, gbase=gbase))

    for c0 in (1024, 1536, 2048, 3072, 2560, 3584, 0, 512):
        add_simple(w_in_d, c0, CV_GMIX)
    for i in range(2):
        add_simple(w_in_d, 4096 + 512 * i, CV_GMIX)
        add_simple(w_a_d, 512 * i)
    for i in range(2):
        add_simple(w_in_d, 5120 + 512 * i, CV_GMIX)
        add_simple(w_b_d, 512 * i)
    for i in range(2):
        add_simple(w_o_d, 512 * i)
    sple = w_ple_d.rearrange("(kc p) n -> p kc n", p=128)
    blocks.append(dict(nk=2, ncol=1024, gbase=None,
                       pieces=[(kc, 1, 0, 1024, (lambda a, b, _s=sple: _s[:, a:a + b, :])) for kc in range(2)]))
    for i in range(11):
        sg = wsrc(w_up_d, 256 * i, 256)
        sv = wsrc(w_up_d, FF + 256 * i, 256)
        pieces = []
        for kc in range(0, 8, 4):
            pieces.append((kc, 4, 0, 256, sg))
            pieces.append((kc, 4, 256, 256, sv))
        blocks.append(dict(nk=8, ncol=512, pieces=pieces, gbase=CV_GFFN))
    for tsg in range(2):
        for half in range(2):
            for (k0, nk) in ((0, 8), (8, 8), (16, 6)):
                add_simple(w_dn_d, 512 * half, None, nk=nk, k0=k0)
                if tsg == 1:
                    blocks[-1]["scr"] = len(blocks) - 1 - 6
    for i in range(2):
        add_simple(w_pg_d, 512 * i, CV_GPG)
    assert len(blocks) == NBLK

    sem_names = ["pe", "act", "dve", "pool", "cst", "xl0", "xl1", "pl", "os"] + \
        ["wl%d" % i for i in range(NSLOT)] + ["cvb%d" % i for i in range(NBLK)]
    sems = {}

    nblk_total = NBLK * ntile
    wstate = dict(issued=0, released=-1, stg=0, ceng=0)

    def slot_view(s, blk):
        nk, ncol = blk["nk"], blk["ncol"]
        return WS[s].ap[:, 0:nk * ncol].rearrange("p (k n) -> p k n", k=nk)

    conv_issued = [0]

    def record_convert(b):
        blk = blocks[b]
        if blk.get("scr", b) != b:
            return
        nk, ncol = blk["nk"], blk["ncol"]
        dstv = scr_d[b, :, 0:nk * ncol].rearrange("p (k n) -> p k n", k=nk)
        seen = set()
        for (kc0, nkc, col0, ncl, src) in blk["pieces"]:
            if (col0, ncl) in seen:
                continue
            seen.add((col0, ncl))
            sap = src(0, nk) if blk["nk"] != 2 or ncol != 1024 else src(0, 2)
            P.dma("cvb%d" % b, (lambda e, _o=dstv[:, :, col0:col0 + ncl], _i=sap: e.dma_start(out=_o, in_=_i)),
                  reads=[], writes=[], eng="pool")
        P.lastw[("dr", b)] = ("cvb%d" % b, P.dcnt["cvb%d" % b], "dma")
        P.readers[("dr", b)] = []

    def pump_convert(upto):
        while conv_issued[0] < min(upto, NBLK):
            record_convert(conv_issued[0])
            conv_issued[0] += 1

    def record_load(n):
        tile_i, b = divmod(n, NBLK)
        s = n % NSLOT
        blk = blocks[b]
        nel = blk["nk"] * blk["ncol"]
        srange = (WS[s].off, nel * 2)
        if tile_i == 0:
            pump_convert(b + 9)
        sb = blk.get("scr", b)
        P.dma("wl%d" % s, (lambda e, _o=WS[s].ap[:, 0:nel], _i=scr_d[sb, :, 0:nel]: e.dma_start(out=_o, in_=_i)),
              reads=[("dr", sb)], writes=[srange])

    def pump():
        while wstate["issued"] < nblk_total and wstate["issued"] <= wstate["released"] + NSLOT:
            record_load(wstate["issued"])
            wstate["issued"] += 1

    wnext = [0]

    def get_block():
        n = wnext[0]
        wnext[0] += 1
        pump()
        assert wstate["issued"] > n, "weight block not loadable (slot not released)"
        blk = blocks[n % NBLK]
        s = n % NSLOT
        return n, slot_view(s, blk), (WS[s].off, blk["nk"] * blk["ncol"] * 2)

    def release(n):
        assert n == wstate["released"] + 1, (n, wstate["released"])
        wstate["released"] = n
        pump()

    def mm_group(bk, pairs, reads, out_ap=None, first=True, last=True, tag=None):
        o = bank(bk) if out_ap is None else out_ap
        fns = []
        n = len(pairs)
        for i, (l, r) in enumerate(pairs):
            fns.append(lambda e, _l=l, _r=r, _s=(first and i == 0), _t=(last and i == n - 1), _o=o:
                       e.matmul(_o, lhsT=_l, rhs=_r, start=_s, stop=_t))
        P.op("pe", fns, reads=reads, writes=[("ps", bk)], fresh=first, tag=tag)

    def norm_stats(xbuf, ts, kind):
        ss, t4, rs = NSC[kind][ts]
        P.op("act", (lambda e: e.activation(out=JUNK.ap, in_=xbuf[:, ts, :], func=AF.Square, accum_out=ss.ap)),
             reads=[xbuf.r(ts)], writes=[JUNK.r(), ss.r()])
        P.op("dve", (lambda e: e.tensor_scalar(out=t4.ap, in0=ss.ap, scalar1=1.0 / D, scalar2=EPS_RMS,
                                               op0=ALU.mult, op1=ALU.add)), reads=[ss.r()], writes=[t4.r()])
        P.op("pool", (lambda e: e.tensor_tensor(out=rs.ap, in0=t4.ap, in1=MHALF.ap[:, 0:1], op=ALU.pow)),
             reads=[t4.r(), MHALF.r()], writes=[rs.r()])

    def ht_ts(ts):
        return [(HT.off + c * 1024 + ts * 256, 256) for c in range(8)]

    def norm_scale(xbuf, ts, kind):
        rs = NSC[kind][ts][2]
        hb = HBT[ts % 2]
        P.op("dve", (lambda e: e.tensor_scalar(out=hb.ap, in0=xbuf[:, ts, :], scalar1=rs.ap, scalar2=None, op0=ALU.mult)),
             reads=[xbuf.r(ts), rs.r()], writes=[hb.r()])

    def norm_apply(xbuf, ts, kind, tag, scale=True):
        hb = HBT[ts % 2]
        if scale:
            norm_scale(xbuf, ts, kind)
        bk = next_bank()
        pv = bank_bf(bk).rearrange("p (c t) -> p c t", c=8)
        fns = [(lambda e, c=c: e.transpose(pv[:, c, :], hb.ap[:, c * 128:(c + 1) * 128], IDENT.ap)) for c in range(8)]
        P.op("pe", fns, reads=[hb.r(), IDENT.r()], writes=[("ps", bk)], fresh=True, tag=tag)
        gb = {"n1a": CV_GMIX, "n1b": CV_GMIX, "n2": CV_GFFN, "n3": CV_GPG}[kind]
        for c in range(8):
            if c < 4 or not OPT_D:
                P.op("act", (lambda e, c=c: e.activation(out=HT.ap[:, c, ts * 128:(ts + 1) * 128], in_=pv[:, c, :], func=AF.Identity,
                                                         scale=cvcol(gb, c))),
                     reads=[("ps", bk), CV.r()], writes=[(HT.off + c * 1024 + ts * 256, 256)])
            else:
                P.op("dve", (lambda e, c=c: e.tensor_scalar(out=HT.ap[:, c, ts * 128:(ts + 1) * 128], in0=pv[:, c, :],
                                                            scalar1=cvcol(gb, c), scalar2=None, op0=ALU.mult)),
                     reads=[("ps", bk), CV.r()], writes=[(HT.off + c * 1024 + ts * 256, 256)])

    pump_convert(9)
    cst_evs = []

    def cdma(out_ap, in_ap, wr):
        cst_evs.append(P.dma("cst", (lambda e, _o=out_ap, _i=in_ap: e.dma_start(out=_o, in_=_i)), reads=[], writes=[wr]))

    cdma(CV.ap, cv_d[:, :], CV.r())
    cdma(GPLEB.ap, rows_d[2:3, :].partition_broadcast(128), GPLEB.r())
    cdma(GFINB.ap, rows_d[3:4, :].partition_broadcast(128), GFINB.r())
    cdma(arena[:, BSB.off:BSB.off + 4096].bitcast(F32), bs_d[0:1, :].partition_broadcast(128), BSB.r())
    wst_stage = arena[:, GV.off:GV.off + 4096].bitcast(F32).rearrange("p (g t) -> p g t", g=8)
    cdma(wst_stage, wst_d[:, :, :], (GV.off, 4096))
    fin = P.dcnt["cst"]
    for c in list(P.lastw):
        ev = P.lastw[c]
        if ev[0] == "cst":
            P.lastw[c] = ("cst", fin, "dma")
    P.op("pool", (lambda e: e.memset(ONES.ap, 1.0)), reads=[], writes=[ONES.r()])
    P.op("pool", (lambda e: e.memset(ONESB.ap, 1.0)), reads=[], writes=[ONESB.r()])
    P.op("pool", (lambda e: e.memset(MHALF.ap, -0.5)), reads=[], writes=[MHALF.r()])
    P.op("pool", (lambda e: e.memset(EPSB.ap, EPS_LN)), reads=[], writes=[EPSB.r()])
    P.op("pool", (lambda e: e.affine_select(out=IDENT.ap, in_=ONESB.ap, pattern=[[-1, 128]], compare_op=ALU.is_equal,
                                            fill=0.0, base=0, channel_multiplier=1)),
         reads=[ONESB.r()], writes=[IDENT.r()])
    wsm = arena[:, GV.off + 4096:GV.off + 8192].bitcast(F32).rearrange("p (g t) -> p g t", g=8)
    P.op("pool", (lambda e: e.affine_select(out=wsm, in_=wst_stage, pattern=[[0, 8], [1, 128]], compare_op=ALU.is_ge,
                                            fill=0.0, base=0, channel_multiplier=-1)),
         reads=[(GV.off, 4096)], writes=[(GV.off + 4096, 4096)])
    P.op("dve", (lambda e: e.tensor_copy(out=WST.ap, in_=wsm)), reads=[(GV.off + 4096, 4096)], writes=[WST.r()])
    wsm2 = arena[:, GV.off + 4096:GV.off + 8192].bitcast(F32)
    for h in range(2):
        P.op("pe", [lambda e, h=h: e.matmul(bank(h), lhsT=ONES.ap, rhs=wsm2[:, h * 512:(h + 1) * 512], start=True, stop=True)],
             reads=[(GV.off + 4096, 4096), ONES.r()], writes=[("ps", h)])
    for g in range(8):
        P.op("dve", (lambda e, g=g: e.scalar_tensor_tensor(out=BSB.ap[:, g, :], in0=bank(g // 4)[:, (g % 4) * 128:(g % 4 + 1) * 128],
                                                          scalar=cvcol(CV_LNVB, g), in1=BSB.ap[:, g, :], op0=ALU.mult, op1=ALU.add)),
             reads=[("ps", g // 4), BSB.r(g), CV.r()], writes=[BSB.r(g)])
    onesrow = arena[0:1, ONES.off:ONES.off + 512].bitcast(F32)

    out_evs = []

    def load_x(ti):
        xb = XBS[ti % 2]
        P.dma("xl%d" % (ti % 2), (lambda e: e.dma_start(out=xb.ap, in_=x_d[ti * T:(ti + 1) * T, :].rearrange("(ts p) d -> p ts d", p=128))),
              reads=[], writes=[xb.r()])

    def n1kind(ti):
        return "n1a" if ti % 2 == 0 else "n1b"

    def load_p(ti):
        P.dma("pl", (lambda e: e.dma_start(out=PIN.ap, in_=p_d[ti * T:(ti + 1) * T, :].rearrange("(ts p) d -> p ts d", p=128))),
              reads=[], writes=[PIN.r()])
        P.op("dve", (lambda e: e.tensor_copy(out=PBF.ap, in_=PIN.ap)), reads=[PIN.r()], writes=[PBF.r()])

    load_x(0)
    load_p(0)
    for ts in range(4):
        norm_stats(XBS[0], ts, n1kind(0))
    for ts in range(4):
        norm_apply(XBS[0], ts, n1kind(0), "n1T")

    for ti in range(ntile):
        tt = ti % tps
        r0 = ti * T
        XB = XBS[ti % 2]
        if tt == 0:
            P.op("pool", (lambda e: e.memset(GLUT.ap[:, :, 0:30], 0.0)), reads=[], writes=[GLUT.r()])
            P.op("pool", (lambda e: e.memset(HC.ap, 0.0)), reads=[], writes=[HC.r()])
        if ti + 1 < ntile:
            load_x(ti + 1)

        vb = [get_block(), get_block()]

        def v_ts(ts):
            gv = GV.ap[:, ts % 2, :]
            for half in range(2):
                n, wv, wr = vb[half]
                bk = (6 + half) if ts == 0 else next_bank()
                mm_group(bk, [(HT.ap[:, kc, ts * 128:(ts + 1) * 128], wv[:, kc, :]) for kc in range(8)],
                         reads=ht_ts(ts) + [wr], tag="v")
                P.op("act", (lambda e, gv=gv, half=half, bk=bk: e.activation(out=gv[:, half * 512:(half + 1) * 512],
                                                                             in_=bank(bk), func=AF.Gelu)),
                     reads=[("ps", bk)], writes=[GV.r(ts % 2)])
            bn, mv, lv = BN[ts % 2], MV[ts % 2], LV[ts % 2]
            for half in range(2):
                P.op("dve", (lambda e, gv=gv, half=half, bn=bn: e.bn_stats(out=bn.ap[:, half * 6:(half + 1) * 6],
                                                                          in_=gv[:, half * 512:(half + 1) * 512])),
                     reads=[GV.r(ts % 2)], writes=[bn.r()])
            P.op("dve", (lambda e, bn=bn, mv=mv: e.bn_aggr(out=mv.ap, in_=bn.ap)), reads=[bn.r()], writes=[mv.r()])
            P.op("dve", (lambda e, mv=mv, lv=lv: e.tensor_scalar(out=lv.ap[:, 0:1], in0=mv.ap[:, 1:2], scalar1=EPS_LN,
                                                              scalar2=None, op0=ALU.add)),
                 reads=[mv.r()], writes=[lv.r()])
            P.op("pool", (lambda e, lv=lv: e.tensor_tensor(out=lv.ap[:, 1:2], in0=lv.ap[:, 0:1], in1=MHALF.ap[:, 0:1], op=ALU.pow)),
                 reads=[lv.r(), MHALF.r()], writes=[lv.r()])
            P.op("dve", (lambda e, gv=gv, mv=mv, lv=lv, ts=ts: e.tensor_scalar(out=NBF.ap[:, ts, :], in0=gv, scalar1=mv.ap[:, 0:1],
                                                                            scalar2=lv.ap[:, 1:2], op0=ALU.subtract, op1=ALU.mult)),
                 reads=[GV.r(ts % 2), mv.r(), lv.r()], writes=[NBF.r(ts)])
        for ts in range(4):
            v_ts(ts)
        release(vb[0][0])
        release(vb[1][0])

        bk = next_bank()
        pv = bank_bf(bk).rearrange("p (c t) -> p c t", c=2)
        fns = [(lambda e, ts=ts, kc=kc, pv=pv: e.transpose(pv[:, kc, ts * 128:(ts + 1) * 128], PBF.ap[:, ts, kc * 128:(kc + 1) * 128], IDENT.ap))
               for ts in range(4) for kc in range(2)]
        P.op("pe", fns, reads=[PBF.r(), IDENT.r()], writes=[("ps", bk)], fresh=True, tag="pT")
        P.op("act", (lambda e, pv=pv: e.copy(out=PT.ap, in_=pv)), reads=[("ps", bk)], writes=[PT.r()])

        def build_diag(idx):
            dg = DG[idx % 8]
            P.op("dve", (lambda e: e.tensor_scalar(out=dg.ap, in0=IDENT.ap, scalar1=cvcol(CV_CBW, idx), scalar2=None, op0=ALU.mult)),
                 reads=[IDENT.r(), CV.r()], writes=[dg.r()])

        for blk_i in range(2):
            if blk_i == 1:
                for idx in range(8):
                    build_diag(idx)
            na, wa, wra = get_block()
            ng, wg, wrg = get_block()
            for j in range(4):
                c = blk_i * 4 + j
                bka = 6 if c == 0 else next_bank()
                mm_group(bka, [(wa[:, kc, j * 128:(j + 1) * 128], HT.ap[:, kc, :]) for kc in range(8)], reads=[HT.r(), wra], tag="a")
                bkg = 7 if c == 0 else next_bank()
                mm_group(bkg, [(wg[:, kc, j * 128:(j + 1) * 128], HT.ap[:, kc, :]) for kc in range(8)], reads=[HT.r(), wrg], tag="gl")
                q = c % 2
                P.op("act", (lambda e, q=q, bkg=bkg: e.activation(out=SQ.ap[:, q, :], in_=bank(bkg), func=AF.Sigmoid)),
                     reads=[("ps", bkg)], writes=[SQ.r(q)])
                P.op("dve", (lambda e, q=q, c=c, bka=bka: e.tensor_tensor(out=GLUT.ap[:, c, 30:542], in0=bank(bka), in1=SQ.ap[:, q, :],
                                                                         op=ALU.mult)),
                     reads=[("ps", bka), SQ.r(q)], writes=[GLUT.r(c)])
            release(na)
            release(ng)

        def lnc_pre(c):
            P.op("dve", (lambda e, c=c: e.tensor_tensor(out=CB.ap[:, c, :], in0=CB.ap[:, c, :], in1=RSTD, op=ALU.mult)),
                 reads=[CB.r(c), STAT.r(2)], writes=[CB.r(c)])
            P.op(("pool" if c % 2 == 0 else "dve"), (lambda e, c=c: e.tensor_tensor(out=CB.ap[:, c, :], in0=CB.ap[:, c, :], in1=MEAN, op=ALU.add)),
                 reads=[CB.r(c), STAT.r(0)], writes=[CB.r(c)])

        def lnc_silu(c):
            P.op("act", (lambda e, c=c: e.activation(out=SCT.ap[:, c, :], in_=CB.ap[:, c, :], func=AF.Silu,
                                                     scale=cvcol(CV_LNBG, c), bias=cvcol(CV_LNBB, c))),
                 reads=[CB.r(c), CV.r()], writes=[SCT.r(c)])

        pend_stats = None

        def stats_mm(c, first, last):
            P.op("pe", [lambda e, c=c, first=first, last=last: e.matmul(bank(6), lhsT=ONES.ap, rhs=CB.ap[:, c, :], start=first, stop=last)],
                 reads=[CB.r(c), ONES.r()], writes=[("ps", 6)], tag="stats")
            P.op("pe", [lambda e, c=c, first=first, last=last: e.matmul(bank(7), lhsT=ONES.ap, rhs=SQ.ap[:, c % 2, :], start=first, stop=last)],
                 reads=[SQ.r(c % 2), ONES.r()], writes=[("ps", 7)])
        for c in range(8):
            bk = next_bank()
            for k in range(31):
                dg = DG[(c * 31 + k) % 8]
                if c * 31 + k >= 8:
                    build_diag(c * 31 + k)
                P.op("pe", [lambda e, dg=dg, c=c, k=k, bk=bk: e.matmul(bank(bk), lhsT=dg.ap, rhs=GLUT.ap[:, c, k:k + 512],
                                                                     start=(k == 0), stop=(k == 30))],
                     reads=[dg.r(), GLUT.r(c)], writes=[("ps", bk)], tag=("conv" if k == 0 else None))
            if pend_stats is not None:
                stats_mm(pend_stats, pend_stats == 0, False)
            P.op("act", (lambda e, c=c, bk=bk: e.activation(out=CB.ap[:, c, :], in_=bank(bk), func=AF.Identity, bias=cvcol(CV_CBB, c))),
                 reads=[("ps", bk), CV.r()], writes=[CB.r(c)])
            P.op("act", (lambda e, c=c: e.activation(out=SQ.ap[:, c % 2, :], in_=CB.ap[:, c, :], func=AF.Square)),
                 reads=[CB.r(c)], writes=[SQ.r(c % 2)])
            pend_stats = c
        stats_mm(7, False, True)
        P.op("pool", (lambda e: e.tensor_copy(out=GLUT.ap[:, :, 0:30], in_=GLUT.ap[:, :, 512:542])), reads=[GLUT.r()], writes=[GLUT.r()])
        MEAN, VAR, RSTD = STAT.ap[:, 0, :], STAT.ap[:, 1, :], STAT.ap[:, 2, :]
        P.op("dve", (lambda e: e.tensor_scalar(out=MEAN, in0=bank(6), scalar1=1.0 / D, scalar2=None, op0=ALU.mult)),
             reads=[("ps", 6)], writes=[STAT.r(0)])
        if OPT_SQ:
            P.op("act", (lambda e: e.activation(out=VAR, in_=bank(6), func=AF.Square, scale=1.0 / D)), reads=[("ps", 6)], writes=[STAT.r(1)])
        else:
            P.op("dve", (lambda e: e.tensor_tensor(out=VAR, in0=MEAN, in1=MEAN, op=ALU.mult)), reads=[STAT.r(0)], writes=[STAT.r(1)])
        P.op("dve", (lambda e: e.scalar_tensor_tensor(out=VAR, in0=bank(7), scalar=1.0 / D, in1=VAR, op0=ALU.mult, op1=ALU.subtract)),
             reads=[("ps", 7), STAT.r(1)], writes=[STAT.r(1)])
        P.op("act", (lambda e: e.activation(out=VAR, in_=VAR, func=AF.Sqrt, bias=EPSB.ap, scale=1.0)),
             reads=[STAT.r(1), EPSB.r()], writes=[STAT.r(1)])
        P.op("dve", (lambda e: e.reciprocal(out=RSTD, in_=VAR)), reads=[STAT.r(1)], writes=[STAT.r(2)])
        P.op("dve", (lambda e: e.scalar_tensor_tensor(out=MEAN, in0=MEAN, scalar=-1.0, in1=RSTD, op0=ALU.mult, op1=ALU.mult)),
             reads=[STAT.r(0), STAT.r(2)], writes=[STAT.r(0)])

        def mix_g(g):
            bk = (6 + g) if g < 2 else next_bank()
            fns = []
            for ts in range(4):
                o = bank(bk)[:, ts * 128:(ts + 1) * 128]
                fns.append(lambda e, o=o, g=g, ts=ts: e.matmul(o, lhsT=NBF.ap[:, ts, g * 128:(g + 1) * 128], rhs=WST.ap[:, g, :],
                                                              start=True, stop=True))
            P.op("pe", fns, reads=[NBF.r(), WST.r()], writes=[("ps", bk)], fresh=True, tag="mix")
            q = g % 2
            bv = bank(bk).rearrange("p (a t) -> p a t", a=4)
            tv = SQ.ap[:, q, :].rearrange("p (a t) -> p a t", a=4)
            bb = BSB.ap[:, g:g + 1, :].broadcast_to([128, 4, 128])
            P.op("dve", (lambda e, bv=bv, tv=tv, bb=bb, g=g: e.scalar_tensor_tensor(out=tv, in0=bv, scalar=cvcol(CV_LNVG, g), in1=bb,
                                                                                   op0=ALU.mult, op1=ALU.add)),
                 reads=[("ps", bk), BSB.r(g), CV.r()], writes=[SQ.r(q)])
            P.op("dve", (lambda e, g=g, q=q: e.tensor_tensor(out=UMT.ap[:, g, :], in0=SQ.ap[:, q, :], in1=GUT.ap[:, g, :], op=ALU.mult)),
                 reads=[SQ.r(q), GUT.r(g)], writes=[UMT.r(g)])

        def u_blk(blk_i, with_mix):
            n, wv, wr = get_block()
            for j in range(4):
                bk = next_bank()
                mm_group(bk, [(wv[:, kc, j * 128:(j + 1) * 128], HT.ap[:, kc, :]) for kc in range(8)], reads=[HT.r(), wr], tag="u")
                c = blk_i * 4 + j
                P.op("act", (lambda e, c=c, bk=bk: e.activation(out=GUT.ap[:, c, :], in_=bank(bk), func=AF.Gelu)),
                     reads=[("ps", bk)], writes=[GUT.r(c)])
                if with_mix:
                    mix_g(j)
                    if j >= 1:
                        mix_g(4 + j - 1)
            return n

        for c in range(4):
            lnc_pre(c)
        release(u_blk(0, False))
        release(u_blk(1, True))
        mix_g(7)
        for c in range(4, 8):
            lnc_pre(c)

        def out_branch(ACT_T, is_b):
            for blk_i in range(2):
                if blk_i == 1 and not is_b:
                    for c in range(8):
                        lnc_silu(c)
                ngt, wgt, wrgt = get_block()
                nw, ww, wrw = get_block()

                def gate(j):
                    bg = next_bank()
                    mm_group(bg, [(wgt[:, kc, j * 128:(j + 1) * 128], HT.ap[:, kc, :]) for kc in range(8)], reads=[HT.r(), wrgt], tag="gate")
                    return bg

                bgs = {0: gate(0), 1: gate(1), 2: gate(2)}
                for j in range(4):
                    m = blk_i * 4 + j
                    bg = bgs[j]
                    by = next_bank()
                    if m == 0:
                        mm_group(by, [(ww[:, kc, j * 128:(j + 1) * 128], ACT_T.ap[:, kc, :]) for kc in range(6)], reads=[ACT_T.r(0, 6), wrw],
                                 first=True, last=False, tag="yout")
                        mm_group(by, [(ww[:, kc, j * 128:(j + 1) * 128], ACT_T.ap[:, kc, :]) for kc in range(6, 8)], reads=[ACT_T.r(6, 2), wrw],
                                 first=False, last=True, tag="yout")
                    else:
                        mm_group(by, [(ww[:, kc, j * 128:(j + 1) * 128], ACT_T.ap[:, kc, :]) for kc in range(8)], reads=[ACT_T.r(), wrw], tag="yout")
                    if j + 3 < 4:
                        bgs[j + 3] = gate(j + 3)
                    q = m % 2
                    P.op("act", (lambda e, q=q, bg=bg: e.activation(out=SG.ap[:, q, :], in_=bank(bg), func=AF.Sigmoid)),
                         reads=[("ps", bg)], writes=[SG.r(q)])
                    if not is_b:
                        P.op("dve", (lambda e, q=q, m=m, by=by: e.tensor_tensor(out=T1.ap[:, m, :], in0=bank(by), in1=SG.ap[:, q, :], op=ALU.mult)),
                             reads=[("ps", by), SG.r(q)], writes=[T1.r(m)])
                    else:
                        P.op("dve", (lambda e, q=q, m=m, by=by: e.tensor_tensor(out=STAT.ap[:, q, :], in0=bank(by), in1=SG.ap[:, q, :], op=ALU.mult)),
                             reads=[("ps", by), SG.r(q)], writes=[STAT.r(q)])
                        P.op("dve", (lambda e, q=q, m=m: e.tensor_tensor(out=MGT.ap[:, m, :], in0=T1.ap[:, m, :], in1=STAT.ap[:, q, :], op=ALU.add)),
                             reads=[T1.r(m), STAT.r(q)], writes=[MGT.r(m)])
                release(ngt)
                release(nw)

        out_branch(UMT, False)

        out_branch(SCT, True)

        wo = [get_block(), get_block()]
        for ts in range(4):
            wbk = {}
            if ts == 0:
                for half in range(2):
                    n, wv, wr = wo[half]
                    wbk[half] = next_bank()
                    mm_group(wbk[half], [(MGT.ap[:, kc, 0:128], wv[:, kc, :]) for kc in range(6)], reads=[MGT.r(0, 6), wr],
                             first=True, last=False, tag="wo")
            for half in range(2):
                n, wv, wr = wo[half]
                if ts == 0:
                    bk = wbk[half]
                    mm_group(bk, [(MGT.ap[:, kc, 0:128], wv[:, kc, :]) for kc in range(6, 8)], reads=[MGT.r(6, 2), wr],
                             first=False, last=True, tag="wo")
                else:
                    bk = next_bank()
                    mm_group(bk, [(MGT.ap[:, kc, ts * 128:(ts + 1) * 128], wv[:, kc, :]) for kc in range(8)], reads=[MGT.r(), wr], tag="wo")
                xs = XB.ap[:, ts, half * 512:(half + 1) * 512]
                P.op("dve", (lambda e, xs=xs, bk=bk: e.tensor_tensor(out=xs, in0=xs, in1=bank(bk), op=ALU.add)),
                     reads=[("ps", bk), XB.r(ts)], writes=[XB.r(ts)])
            norm_stats(XB, ts, "n2")
            if ts >= 1:
                norm_scale(XB, ts - 1, "n2")
            if ts >= 2:
                norm_apply(XB, ts - 2, "n2", "n2T", scale=False)
        release(wo[0][0])
        release(wo[1][0])

        norm_scale(XB, 3, "n2")
        norm_apply(XB, 2, "n2", "n2T", scale=False)
        n, wv, wr = get_block()
        for ts in range(4):
            for half in range(2):
                bk = next_bank()
                mm_group(bk, [(PT.ap[:, kc, ts * 128:(ts + 1) * 128], wv[:, kc, half * 512:(half + 1) * 512]) for kc in range(2)],
                         reads=[PT.r(), wr], tag="ple")
                P.op("act", (lambda e, ts=ts, half=half, bk=bk: e.copy(out=PEN.ap[:, ts, half * 512:(half + 1) * 512], in_=bank(bk))),
                     reads=[("ps", bk)], writes=[PEN.r(ts)])
        release(n)
        norm_apply(XB, 3, "n2", "n2T", scale=False)
        if ti + 1 < ntile:
            load_p(ti + 1)

        for i in range(11):
            n, wv, wr = get_block()
            for q in range(2):
                j = 2 * i + q
                accs = []
                for which in range(2):
                    jj = j + which * NFC
                    bk = next_bank()
                    co = which * 256 + q * 128
                    mm_group(bk, [(wv[:, kc, co:co + 128], HT.ap[:, kc, :]) for kc in range(8)], reads=[HT.r(), wr], tag="up")
                    a = (q * 2 + which)
                    acc = ACC.ap[:, a, :]
                    accs.append(acc)
                    w0, w1, w2 = (cvcol(CV_FCW, k * 44 + jj) for k in range(3))
                    P.op("act", (lambda e, acc=acc, bk=bk, w2=w2, jj=jj: e.activation(out=acc, in_=bank(bk), func=AF.Identity,
                                                                                   scale=w2, bias=cvcol(CV_FCB, jj))),
                         reads=[("ps", bk), CV.r()], writes=[ACC.r(a)])
                    P.op("dve", (lambda e, acc=acc, bk=bk, w1=w1: e.scalar_tensor_tensor(out=acc[:, 1:512], in0=bank(bk)[:, 0:511], scalar=w1,
                                                                                       in1=acc[:, 1:512], op0=ALU.mult, op1=ALU.add)),
                         reads=[("ps", bk), ACC.r(a), CV.r()], writes=[ACC.r(a)])
                    P.op("dve", (lambda e, acc=acc, bk=bk, w0=w0: e.scalar_tensor_tensor(out=acc[:, 2:512], in0=bank(bk)[:, 0:510], scalar=w0,
                                                                                       in1=acc[:, 2:512], op0=ALU.mult, op1=ALU.add)),
                         reads=[("ps", bk), ACC.r(a), CV.r()], writes=[ACC.r(a)])
                    P.op(("pool" if OPT_TINY else "dve"), (lambda e, acc=acc, jj=jj: e.tensor_tensor(out=acc[:, 0:2], in0=acc[:, 0:2], in1=HC.ap[:, :, jj], op=ALU.add)),
                         reads=[ACC.r(a), HC.r()], writes=[ACC.r(a)])
                    P.op("dve", (lambda e, bk=bk, jj=jj: e.tensor_copy(out=HAL.ap[:, :, jj], in_=bank(bk)[:, 510:512])),
                         reads=[("ps", bk)], writes=[HAL.r()])
                P.op("act", (lambda e, q=q, acc=accs[0]: e.activation(out=GEL.ap[:, q, :], in_=acc, func=AF.Gelu)),
                     reads=[ACC.r(q * 2)], writes=[GEL.r(q)])
                P.op("pool", (lambda e, q=q, j=j, acc=accs[1]: e.tensor_tensor(out=HMT.ap[:, j, :], in0=GEL.ap[:, q, :], in1=acc, op=ALU.mult)),
                     reads=[GEL.r(q), ACC.r(q * 2 + 1)], writes=[HMT.r(j)])
            release(n)
        FW0, FW1 = CV.ap[:, CV_FCW:CV_FCW + 44], CV.ap[:, CV_FCW + 44:CV_FCW + 88]
        P.op("dve", (lambda e: e.tensor_tensor(out=HC.ap[:, 0, :], in0=HAL.ap[:, 1, :], in1=FW1, op=ALU.mult)),
             reads=[HAL.r(), CV.r()], writes=[HC.r()])
        P.op("dve", (lambda e: e.tensor_tensor(out=TMPH.ap, in0=HAL.ap[:, 0, :], in1=FW0, op=ALU.mult)),
             reads=[HAL.r(), CV.r()], writes=[TMPH.r()])
        P.op("dve", (lambda e: e.tensor_tensor(out=HC.ap[:, 0, :], in0=HC.ap[:, 0, :], in1=TMPH.ap, op=ALU.add)),
             reads=[HC.r(), TMPH.r()], writes=[HC.r()])
        P.op("dve", (lambda e: e.tensor_tensor(out=HC.ap[:, 1, :], in0=HAL.ap[:, 1, :], in1=FW0, op=ALU.mult)),
             reads=[HAL.r(), CV.r()], writes=[HC.r()])

        for ts in range(4):
            P.op("dve", (lambda e, ts=ts: e.scalar_tensor_tensor(out=JUNKF.ap, in0=PEN.ap[:, ts, :], scalar=1.0, in1=PEN.ap[:, ts, :],
                                                                op0=ALU.mult, op1=ALU.mult, accum_out=SP4.ap[:, ts:ts + 1])),
                 reads=[PEN.r(ts)], writes=[JUNKF.r(), SP4.r()])
        P.op("dve", (lambda e: e.tensor_scalar(out=SP4.ap, in0=SP4.ap, scalar1=1.0 / D, scalar2=EPS_RMS, op0=ALU.mult, op1=ALU.add)),
             reads=[SP4.r()], writes=[SP4.r()])
        P.op("pool", (lambda e: e.tensor_tensor(out=SP4.ap, in0=SP4.ap, in1=MHALF.ap[:, 0:4], op=ALU.pow)),
             reads=[SP4.r(), MHALF.r()], writes=[SP4.r()])
        for ts in range(4):
            P.op("dve", (lambda e, ts=ts: e.scalar_tensor_tensor(out=PEN.ap[:, ts, :], in0=PEN.ap[:, ts, :], scalar=SP4.ap[:, ts:ts + 1],
                                                                in1=GPLEB.ap, op0=ALU.mult, op1=ALU.mult)),
                 reads=[PEN.r(ts), SP4.r(), GPLEB.r()], writes=[PEN.r(ts)])

        if ti + 1 < ntile:
            for ts in range(4):
                norm_stats(XBS[(ti + 1) % 2], ts, n1kind(ti + 1))
        for tsg in range(2):
            tss = (2 * tsg, 2 * tsg + 1)
            for half in range(2):
                if tsg == 0 and half == 0:
                    bks = {tss[0]: 6, tss[1]: 7}
                else:
                    bks = {ts: next_bank() for ts in tss}
                for (k0, nk) in ((0, 8), (8, 8), (16, 6)):
                    n, wv, wr = get_block()
                    for ts in tss:
                        mm_group(bks[ts], [(HMT.ap[:, k0 + kc, ts * 128:(ts + 1) * 128], wv[:, kc, :]) for kc in range(nk)],
                                 reads=[HMT.r(k0, nk), wr], first=(k0 == 0), last=(k0 == 16), tag="down")
                        if k0 == 16:
                            xs = XB.ap[:, ts, half * 512:(half + 1) * 512]
                            P.op("dve", (lambda e, xs=xs, bk=bks[ts]: e.tensor_tensor(out=xs, in0=xs, in1=bank(bk), op=ALU.add)),
                                 reads=[("ps", bks[ts]), XB.r(ts)], writes=[XB.r(ts)])
                            if half == 1:
                                norm_stats(XB, ts, "n3")
                    release(n)
                if tsg == 1 and half == 0:
                    norm_apply(XB, 0, "n3", "n3T", scale=False)
                    norm_apply(XB, 1, "n3", "n3T", scale=False)
            if tsg == 0:
                norm_scale(XB, 0, "n3")
                norm_scale(XB, 1, "n3")
        wpg = [get_block(), get_block()]

        def pg_ts(ts):
            for half in range(2):
                n, wv, wr = wpg[half]
                bk = next_bank()
                mm_group(bk, [(HT.ap[:, kc, ts * 128:(ts + 1) * 128], wv[:, kc, :]) for kc in range(8)], reads=ht_ts(ts) + [wr], tag="pg")
                q = (ts * 2 + half) % 2
                P.op("act", (lambda e, q=q, bk=bk: e.activation(out=SGP.ap[:, q, :], in_=bank(bk), func=AF.Sigmoid)),
                     reads=[("ps", bk)], writes=[SGP.r(q)])
                P.op("dve", (lambda e, q=q, ts=ts, half=half: e.tensor_tensor(out=SGP.ap[:, q, :], in0=SGP.ap[:, q, :],
                                                                             in1=PEN.ap[:, ts, half * 512:(half + 1) * 512], op=ALU.mult)),
                     reads=[SGP.r(q), PEN.r(ts)], writes=[SGP.r(q)])
                xs = XB.ap[:, ts, half * 512:(half + 1) * 512]
                P.op("dve", (lambda e, xs=xs, q=q: e.tensor_tensor(out=xs, in0=xs, in1=SGP.ap[:, q, :], op=ALU.add)),
                     reads=[SGP.r(q), XB.r(ts)], writes=[XB.r(ts)])

        norm_scale(XB, 2, "n3")
        norm_scale(XB, 3, "n3")
        pg_ts(0)
        norm_apply(XB, 2, "n3", "n3T", scale=False)
        pg_ts(1)
        norm_apply(XB, 3, "n3", "n3T", scale=False)
        nxt = ti + 1 < ntile
        XN = XBS[(ti + 1) % 2]
        nk1 = n1kind(ti + 1)
        if nxt:
            norm_scale(XN, 0, nk1)
            norm_scale(XN, 1, nk1)
        pg_ts(2)
        if nxt:
            norm_apply(XN, 0, nk1, "n1T", scale=False)
            norm_apply(XN, 1, nk1, "n1T", scale=False)
            norm_scale(XN, 2, nk1)
            norm_scale(XN, 3, nk1)
        pg_ts(3)
        release(wpg[0][0])
        release(wpg[1][0])
        if nxt:
            norm_apply(XN, 2, nk1, "n1T", scale=False)
            norm_apply(XN, 3, nk1, "n1T", scale=False)

        for ts in range(4):
            norm_stats(XB, ts, "fin")
        for ts in range(4):
            rs = NSC["fin"][ts][2]
            P.op("dve", (lambda e, ts=ts, rs=rs, XB=XB: e.scalar_tensor_tensor(out=OUTB.ap[:, ts, :], in0=XB.ap[:, ts, :], scalar=rs.ap,
                                                                       in1=GFINB.ap, op0=ALU.mult, op1=ALU.mult)),
                 reads=[XB.r(ts), rs.r(), GFINB.r()], writes=[OUTB.r(ts)])
        out_evs.append(P.dma("os", (lambda e, r0=r0: e.dma_start(out=out_d[r0:r0 + T, :].rearrange("(ts p) d -> p ts d", p=128), in_=OUTB.ap)),
                             reads=[OUTB.r()], writes=[]))

    fin_evs = [(s, v, "dma") for s, v in P.dcnt.items()]
    P.wait_all("sp", fin_evs)

    build_nc.last_prog = P
    import contextlib
    with contextlib.ExitStack() as es:
        for nme in sem_names:
            sems[nme] = es.enter_context(nc.semaphore("s_" + nme))
        block = es.enter_context(nc.Block())

        def replay(e, key):
            for waits, fns, inc, tag in P.ops[key]:
                for sem, val in waits:
                    e.wait_ge(sems[sem], val)
                for i, fn in enumerate(fns):
                    _check_snap(fn)
                    ins = fn(e)
                    if key == "pe":
                        P.pe_names.append((ins.ins.name, tag))
                    if tag is not None and i == 0:
                        ins.annotate(tag)
                    if inc is not None and i == len(fns) - 1:
                        ins.then_inc(sems[inc[0]], inc[1])

        @block.sync
        def _(e):
            replay(e, "sp")

        @block.tensor
        def _(e):
            replay(e, "pe")

        @block.scalar
        def _(e):
            replay(e, "act")

        @block.vector
        def _(e):
            replay(e, "dve")

        @block.gpsimd
        def _(e):
            replay(e, "pool")
    return nc


def prep_shared(inp):
    f = lambda a: np.ascontiguousarray(np.asarray(a, dtype=np.float32))
    col8 = lambda v: f(v).reshape(8, 128).T
    cv = np.concatenate([
        col8(inp["g_mix"][0]), col8(inp["g_ffn"][0]), col8(inp["g_pg"][0]),
        col8(inp["ln_b_g"][0]), col8(inp["ln_b_b"][0]), col8(inp["conv_b_b"][0]),
        f(inp["conv_b_w"][0]).reshape(31, 8, 128).transpose(2, 1, 0).reshape(128, 8 * 31),
        f(inp["ffn_conv_w"][0]).reshape(3, 44, 128).transpose(2, 0, 1).reshape(128, 3 * 44),
        f(inp["ffn_conv_b"][0]).reshape(44, 128).T,
        col8(inp["ln_v_g"][0]), col8(inp["ln_v_b"][0]),
    ], axis=1)
    assert cv.shape == (128, NCV)
    rows = np.stack([f(inp["ln_v_g"][0]), f(inp["ln_v_b"][0]), f(inp["g_ple"][0]), f(inp["g_final"])], axis=0)
    return {
        "w_in": f(inp["w_in"][0]), "w_a_out": f(inp["w_a_out"][0]), "w_b_out": f(inp["w_b_out"][0]),
        "w_o": f(inp["w_o"][0]), "w_up": f(inp["w_up"][0]), "w_down": f(inp["w_down"][0]),
        "w_pg": f(inp["w_pg"][0]), "w_ple": f(inp["w_ple"][0]),
        "cv": f(cv), "rows": f(rows), "bs": f(inp["b_s"][0]).reshape(1, 1024),
        "wst": f(np.transpose(f(inp["w_s"][0]), (2, 0, 1))),
    }


def kernel(**inputs):
    x = np.asarray(inputs["x"], dtype=np.float32)
    p = np.asarray(inputs["p"], dtype=np.float32)[0]
    bsz, seq, _ = x.shape
    ncores = 8
    n_seq = bsz // ncores
    tps = seq // T
    nc = build_nc(n_seq, tps)
    shared = prep_shared(inputs)
    in_maps = []
    for c in range(ncores):
        m = dict(shared)
        m["x"] = np.ascontiguousarray(x[c * n_seq:(c + 1) * n_seq].reshape(n_seq * seq, D))
        m["p"] = np.ascontiguousarray(p[c * n_seq:(c + 1) * n_seq].reshape(n_seq * seq, 256))
        in_maps.append(m)
    res = run_bass_kernel_spmd(nc, in_maps, core_ids=list(range(ncores)))
    out = np.concatenate([r["out"].reshape(n_seq, seq, D) for r in res.results], axis=0)
    return out.astype(np.float32)
```
